# Optimizing a Trainium2 kernel written in Bass

```python
import jax, jax.numpy as jnp
from jax import lax
import numpy as np

D_MODEL = 1024
BATCH = 16
SEQ = 4096
DEPTH = 4

MIX_WIDTH = D_MODEL
N_HEADS = 8
QK_NOPE_DIM = 64
QK_ROPE_DIM = 32
V_HEAD_DIM = 64
Q_LORA_RANK = 384
KV_LORA_RANK = 256
ATTN_WIDTH = N_HEADS * V_HEAD_DIM
ROPE_THETA = 10000.0
Q_BLOCK = 128
CONV_WIDTH = MIX_WIDTH - ATTN_WIDTH
CONV_GROUPS = 8
CONV_K = 3
PROJ_SPLITS = (Q_LORA_RANK, KV_LORA_RANK, QK_ROPE_DIM, CONV_WIDTH, CONV_WIDTH, CONV_WIDTH)
PROJ_WIDTH = sum(PROJ_SPLITS)
D_FF = 2816
FFN_RESIDUAL = 0.5
N_MOD = 9
EPS = 1e-6

kernel_name = "hybrid_mla_shortconv_macaron_adaln"


def rms_norm(x, g):
    xf = x.astype(jnp.float32)
    y = xf * lax.rsqrt(jnp.mean(xf * xf, axis=-1, keepdims=True) + EPS)
    return (y * g.astype(jnp.float32)).astype(x.dtype)


def modulate(h, shift, scale):
    return h * (1 + scale[:, None, :]) + shift[:, None, :]


def swiglu(h, w_gu, w_down):
    g, u = jnp.split(h @ w_gu, 2, axis=-1)
    return (jax.nn.silu(g) * u) @ w_down


def rope_tables(positions):
    inv_freq = 1.0 / (ROPE_THETA ** (jnp.arange(0, QK_ROPE_DIM, 2, dtype=jnp.float32) / QK_ROPE_DIM))
    ang = positions.astype(jnp.float32)[..., None] * inv_freq
    return jnp.cos(ang), jnp.sin(ang)


def apply_rope(x, cos, sin):
    x1, x2 = jnp.split(x.astype(jnp.float32), 2, axis=-1)
    c, s = cos[:, :, None, :], sin[:, :, None, :]
    return jnp.concatenate([x1 * c - x2 * s, x2 * c + x1 * s], axis=-1).astype(x.dtype)


def causal_attention(q, k, v):
    S = q.shape[1]
    scale = (QK_NOPE_DIM + QK_ROPE_DIM) ** -0.5
    outs = []
    for i in range(S // Q_BLOCK):
        kv_len = (i + 1) * Q_BLOCK
        qi = q[:, i * Q_BLOCK:kv_len]
        s = jnp.einsum('bqhd,bkhd->bhqk', qi, k[:, :kv_len]).astype(jnp.float32) * scale
        q_pos = i * Q_BLOCK + jnp.arange(Q_BLOCK)
        mask = jnp.arange(kv_len)[None, :] <= q_pos[:, None]
        p = jax.nn.softmax(jnp.where(mask, s, -jnp.inf), axis=-1)
        outs.append(jnp.einsum('bhqk,bkhd->bqhd', p.astype(v.dtype), v[:, :kv_len]))
    return jnp.concatenate(outs, axis=1)


def causal_short_conv(u, w):
    S = u.shape[1]
    up = jnp.pad(u, ((0, 0), (CONV_K - 1, 0), (0, 0)))
    y = w[0] * up[:, 0:S]
    for j in range(1, CONV_K):
        y = y + w[j] * up[:, j:j + S]
    return y


def hybrid_mixer(h, cos, sin, w_in, q_a_norm, w_uq, kv_a_norm, w_ukv, conv_w,
                 attn_out_norm, conv_out_norm, w_o):
    B, S, _ = h.shape
    proj = h @ w_in
    offs = np.cumsum(PROJ_SPLITS)[:-1].tolist()
    c_q, c_kv, k_pe, gate_b, gate_c, val = jnp.split(proj, offs, axis=-1)
    q = (rms_norm(c_q, q_a_norm) @ w_uq).reshape(B, S, N_HEADS, QK_NOPE_DIM + QK_ROPE_DIM)
    q_nope, q_pe = jnp.split(q, [QK_NOPE_DIM], axis=-1)
    q_pe = apply_rope(q_pe, cos, sin)
    kv = (rms_norm(c_kv, kv_a_norm) @ w_ukv).reshape(B, S, N_HEADS, QK_NOPE_DIM + V_HEAD_DIM)
    k_nope, v = jnp.split(kv, [QK_NOPE_DIM], axis=-1)
    k_pe = apply_rope(k_pe[:, :, None, :], cos, sin)
    q_full = jnp.concatenate([q_nope, q_pe], axis=-1)
    k_full = jnp.concatenate([k_nope, jnp.broadcast_to(k_pe, (B, S, N_HEADS, QK_ROPE_DIM))], axis=-1)
    attn = causal_attention(q_full, k_full, v).reshape(B, S, ATTN_WIDTH)
    conv = gate_b * causal_short_conv(gate_c * val, conv_w)
    merged = jnp.concatenate([rms_norm(attn, attn_out_norm), rms_norm(conv, conv_out_norm)], axis=-1)
    return merged @ w_o


def setup_inputs(seed: int = 0) -> dict:
    key = jax.random.key(seed)
    ks = iter(jax.random.split(key, 32))
    L, D = DEPTH, D_MODEL

    def nrm(shape, scale):
        return jax.random.normal(next(ks), shape, jnp.float32) * scale

    def gain(shape):
        return 1.0 + nrm(shape, 0.05)

    x = nrm((BATCH, SEQ, D), 1.0)
    c = nrm((BATCH, D), 1.0)
    offset = jax.random.randint(next(ks), (BATCH, 1), 0, 4096, dtype=jnp.int32)
    positions = offset + jnp.arange(SEQ, dtype=jnp.int32)[None, :]
    return {
        "x": x, "c": c, "positions": positions,
        "w_ada": nrm((L, D, N_MOD * D), 0.3 * D ** -0.5),
        "b_ada": nrm((L, N_MOD * D), 0.01),
        "ffn1_norm": gain((L, D)),
        "ffn1_w_gu": nrm((L, D, 2 * D_FF), D ** -0.5),
        "ffn1_w_down": nrm((L, D_FF, D), D_FF ** -0.5),
        "mix_norm": gain((L, D)),
        "w_in": nrm((L, D, PROJ_WIDTH), D ** -0.5),
        "q_a_norm": gain((L, Q_LORA_RANK)),
        "w_uq": nrm((L, Q_LORA_RANK, N_HEADS * (QK_NOPE_DIM + QK_ROPE_DIM)), Q_LORA_RANK ** -0.5),
        "kv_a_norm": gain((L, KV_LORA_RANK)),
        "w_ukv": nrm((L, KV_LORA_RANK, N_HEADS * (QK_NOPE_DIM + V_HEAD_DIM)), KV_LORA_RANK ** -0.5),
        "conv_w": nrm((L, CONV_K, CONV_WIDTH), CONV_K ** -0.5),
        "attn_out_norm": gain((L, ATTN_WIDTH)),
        "conv_out_norm": gain((L, CONV_WIDTH)),
        "w_o": nrm((L, MIX_WIDTH, D), MIX_WIDTH ** -0.5),
        "ffn2_norm": gain((L, D)),
        "ffn2_w_gu": nrm((L, D, 2 * D_FF), D ** -0.5),
        "ffn2_w_down": nrm((L, D_FF, D), D_FF ** -0.5),
        "final_norm": gain((D,)),
    }


def reference(x, c, positions, w_ada, b_ada, ffn1_norm, ffn1_w_gu, ffn1_w_down,
              mix_norm, w_in, q_a_norm, w_uq, kv_a_norm, w_ukv, conv_w,
              attn_out_norm, conv_out_norm, w_o, ffn2_norm, ffn2_w_gu, ffn2_w_down,
              final_norm):
    cos, sin = rope_tables(positions)
    c_act = jax.nn.silu(c)
    for l in range(DEPTH):
        mod = jnp.split(c_act @ w_ada[l] + b_ada[l], N_MOD, axis=-1)
        h = modulate(rms_norm(x, ffn1_norm[l]), mod[0], mod[1])
        x = x + FFN_RESIDUAL * (1 + mod[2])[:, None, :] * swiglu(h, ffn1_w_gu[l], ffn1_w_down[l])
        h = modulate(rms_norm(x, mix_norm[l]), mod[3], mod[4])
        y = hybrid_mixer(h, cos, sin, w_in[l], q_a_norm[l], w_uq[l], kv_a_norm[l], w_ukv[l],
                         conv_w[l], attn_out_norm[l], conv_out_norm[l], w_o[l])
        x = x + (1 + mod[5])[:, None, :] * y
        h = modulate(rms_norm(x, ffn2_norm[l]), mod[6], mod[7])
        x = x + FFN_RESIDUAL * (1 + mod[8])[:, None, :] * swiglu(h, ffn2_w_gu[l], ffn2_w_down[l])
    return rms_norm(x, final_norm)
```

```python
import numpy as np
import concourse.bass as bass
import concourse.mybir as mybir
from concourse.bass_utils import run_bass_kernel_spmd

F32 = mybir.dt.float32
BF16 = mybir.dt.bfloat16
I32 = mybir.dt.int32
AF = mybir.ActivationFunctionType
ALU = mybir.AluOpType

NCORES = 8
D = 1024
SEQ = 4096
NSEQ = 2
TOK = NSEQ * SEQ
T = 512
NTS = SEQ // T
NT = NSEQ * NTS
DFF = 2816
NF = 22
L = 4
EPS = 1e-6
NH = 8
WIN_COLS = 2368
WUQ_COLS = 1536
SM_SCALE = float(96 ** -0.5)
DUMN = 384
FILLER = True

V_BADA = 0
V_FFN1N = V_BADA + 288
V_MIXN = V_FFN1N + 32
V_FFN2N = V_MIXN + 32
V_FINALN = V_FFN2N + 32
V_QAN = V_FINALN + 8
V_KVAN = V_QAN + 12
V_CONVW = V_KVAN + 8
V_AON = V_CONVW + 48
V_CON = V_AON + 32
V_CT = V_CON + 16
V_INVF = V_CT + 16
V_SGN = V_INVF + 1
V_HALFPI = V_SGN + 1
V_EPS = V_HALFPI + 1
NV = V_EPS + 1

MAGIC = 12582912.0
TWO_PI = 2.0 * np.pi
C1 = 6.28125
C2 = float(TWO_PI - 6.28125)
PI_LO = 3.1415925


class Op:
    __slots__ = ("eng", "fn", "deps", "sig", "cnt", "is_dma", "dsem", "dval", "qi")


class Sched:
    NSEM = 16

    def __init__(self):
        self.ops = []
        self.lastw = {}
        self.rd = {}
        self.rd_dma = {}

    def add(self, eng, fn, r=(), w=(), dma=False):
        op = Op()
        op.eng = eng
        op.fn = fn
        op.is_dma = dma
        op.sig = False
        op.cnt = 0
        deps = {}
        for k in r:
            o = self.lastw.get(k)
            if o is not None:
                deps[id(o)] = o
        for k in w:
            o = self.lastw.get(k)
            if o is not None:
                deps[id(o)] = o
            for o in self.rd.get(k, {}).values():
                deps[id(o)] = o
            for o in self.rd_dma.get(k, ()):
                deps[id(o)] = o
        op.deps = list(deps.values())
        for k in r:
            if dma:
                self.rd_dma.setdefault(k, []).append(op)
            else:
                self.rd.setdefault(k, {})[eng] = op
        for k in w:
            self.lastw[k] = op
            self.rd[k] = {}
            self.rd_dma[k] = []
        self.ops.append(op)
        return op

    def finalize(self):
        qcount = {}
        for op in self.ops:
            for d in op.deps:
                if d.is_dma:
                    continue
                if d.eng != op.eng or op.is_dma or d.eng != "pe":
                    d.sig = True
            if op.is_dma:
                qi = qcount.get(op.eng, 0)
                qcount[op.eng] = qi + 1
                op.qi = qi
        cnt = {}
        for op in self.ops:
            if op.sig and not op.is_dma:
                cnt[op.eng] = cnt.get(op.eng, 0) + 1
                op.cnt = cnt[op.eng]

    def emit_engine(self, eng, handle, sems, dma_sems):
        waited = {}
        pool = dma_sems.get(eng)
        for op in self.ops:
            if op.eng != eng:
                continue
            for d in op.deps:
                if d.is_dma:
                    s = dma_sems[d.eng][d.qi % self.NSEM]
                    v = 16 * (d.qi // self.NSEM + 1)
                elif d.eng == eng and eng == "pe" and not op.is_dma:
                    continue
                else:
                    s = sems[d.eng]
                    v = d.cnt
                key = id(s)
                if waited.get(key, 0) >= v:
                    continue
                handle.wait_ge(s, v)
                waited[key] = v
            if op.is_dma:
                s = pool[op.qi % self.NSEM]
                if op.qi >= self.NSEM:
                    v = 16 * (op.qi // self.NSEM)
                    if waited.get(id(s), 0) < v:
                        handle.wait_ge(s, v)
                        waited[id(s)] = v
                ins = op.fn(handle)
                ins.then_inc(s, 16)
            else:
                if op.fn is None:
                    continue
                ins = op.fn(handle)
                if op.sig:
                    ins.then_inc(sems[eng], 1)


class Arena:
    def __init__(self, ap, nwords):
        self.ap = ap
        self.n = nwords
        self.off = 0

    def alloc(self, dtype, shape):
        n = 1
        for s in shape:
            n *= s
        words = n if dtype in (F32, I32) else (n + 1) // 2
        a = self.ap[:, self.off:self.off + words]
        self.off += words
        assert self.off <= self.n, f"arena overflow {self.off} > {self.n}"
        if dtype == BF16:
            a = a.bitcast(BF16)
        elif dtype == I32:
            a = a.bitcast(I32)
        if len(shape) == 2:
            a = a.rearrange("p (a b) -> p a b", a=shape[0])
        elif len(shape) == 3:
            a = a.rearrange("p (a b c) -> p a b c", a=shape[0], b=shape[1])
        return a


ARENA_WORDS = 53200


class Builder:
    def __init__(self, cfg):
        self.cfg = cfg
        self.nlayers = cfg.get("layers", L)
        self.ntiles = cfg.get("tiles", NT)
        self.debug = cfg.get("debug", False)
        self.stop = cfg.get("stop", None)
        self.S = Sched()
        self.bank_rr = 0

    def declare_dram(self, nc):
        def inp(name, shape, dt=F32):
            return nc.dram_tensor(name, shape, dt, kind="ExternalInput").ap()

        self.xT = inp("xT", [D, TOK])
        self.pos = inp("pos", [1, TOK], I32)
        NL = self.nlayers
        self.w_ada = inp("w_ada", [NL, D, 9 * D])
        self.wgu = [inp("wgu1", [NL, D, 2 * DFF]), inp("wgu2", [NL, D, 2 * DFF])]
        self.wd = [inp("wd1", [NL, DFF, D]), inp("wd2", [NL, DFF, D])]
        self.w_in = inp("w_in_p", [NL, D, WIN_COLS])
        self.w_uq = inp("w_uq_p", [NL, 384, WUQ_COLS])
        self.w_ukv = inp("w_ukv_p", [NL, 256, 1024])
        self.w_o = inp("w_o", [NL, D, D])
        self.vecs = inp("vecs", [128, NV])
        self.consts = inp("consts", [128, 384])
        self.yT = nc.dram_tensor("yT", [D, TOK], F32, kind="ExternalOutput").ap()
        kind = "ExternalOutput" if self.debug else "Internal"

        def scr(name, shape, dt):
            return nc.dram_tensor(name, shape, dt, kind=kind).ap()

        self.XW = scr("XW", [D, TOK], F32)
        self.QD = scr("QD", [NT, 96, NH * T], BF16)
        self.KD = scr("KD", [NT, 96, NH * T], BF16)
        self.VD = scr("VD", [NT, 128, 4 * 768], BF16)
        self.CD = scr("CD", [NT, 128, 4 * T], BF16)
        self.CT = scr("CT", [32, TOK], F32)
        self.ST = scr("ST", [32, TOK], F32)

    def bank(self):
        b = 1 + (self.bank_rr % 7)
        self.bank_rr += 1
        return b

    def barrier(self):
        self.S.add("pool", lambda e: e.memset(self.SCR1[0:1, 0:1], 0.0), r=(), w=["ARENA"])

    def add(self, eng, fn, r=(), w=(), dma=False):
        r = tuple(r) + ("ARENA",)
        return self.S.add(eng, fn, r=r, w=w, dma=dma)

    def dump(self, name, ap, shape, dt, keys):
        if not self.debug:
            return
        d = self.nc.dram_tensor("DBG_" + name, list(shape), dt, kind="ExternalOutput").ap()
        self.add("sp", lambda e: e.dma_start(out=d, in_=ap), r=keys, w=["DBG_" + name], dma=True)

    def par(self, l, q, k, b):
        return self.PAR[:, ((l * 9 + q) * 8 + k) * 2 + b:((l * 9 + q) * 8 + k) * 2 + b + 1]

    def vcol(self, c, p0=0, p1=128):
        return self.VEC[p0:p1, c:c + 1]

    def rms_stats(self, chunks, nfeat, RS, rs_key, SQ, kparts=128):
        ps0 = self.PS[:, 0, :]
        n = len(chunks)
        for i, (ap, keys) in enumerate(chunks):
            sq = SQ[0:kparts, i % 2, :]
            self.add("act", lambda e, sq=sq, ap=ap: e.activation(sq, ap, AF.Square),
                     r=keys, w=[f"SQ{i % 2}"])
            self.add("pe", lambda e, sq=sq, i=i: e.matmul(ps0, lhsT=self.ONESB[0:kparts, :], rhs=sq,
                                                          start=(i == 0), stop=(i == n - 1)),
                     r=[f"SQ{i % 2}"], w=["ps0"])
        self.rstd_from_psum(ps0, "ps0", nfeat, RS, rs_key)

    def rstd_from_psum(self, ps, ps_key, nfeat, RS, rs_key):
        self.add("act", lambda e: e.activation(RS, ps, AF.Ln, bias=self.vcol(V_EPS), scale=1.0 / nfeat),
                 r=[ps_key, "VEC"], w=[rs_key])
        self.add("act", lambda e: e.activation(RS, RS, AF.Exp, scale=-0.5), r=[rs_key], w=[rs_key])

    def prep(self):
        A = self.arena
        base = A.off
        CACT = A.alloc(BF16, (16,))
        WA = [A.alloc(BF16, (8, 3072)) for _ in range(2)]
        MOD = A.alloc(F32, (72, 2))
        TMPM = A.alloc(F32, (8, 2))
        POSI = A.alloc(I32, (2048,))
        ANG = A.alloc(F32, (2048,))
        KF = A.alloc(F32, (2048,))
        RR = A.alloc(F32, (2048,))
        CC = A.alloc(F32, (2048,))
        SS = A.alloc(F32, (2048,))
        add = self.add
        VEC = self.VEC
        add("sp", lambda e: e.dma_start(out=VEC, in_=self.vecs[:, :]), w=["VEC"], dma=True)
        add("pool", lambda e: e.dma_start(out=self.ONESB, in_=self.consts[:, 0:128]), w=["ONESB"], dma=True)
        add("pool", lambda e: e.dma_start(out=self.IDN, in_=self.consts[:, 128:256]), w=["IDN"], dma=True)
        add("pool", lambda e: e.dma_start(out=self.MB, in_=self.consts[:, 256:384]), w=["MB"], dma=True)
        add("sp", lambda e: e.dma_start(out=self.ONESF, in_=self.consts[:, 0:128]), w=["ONESF"], dma=True)
        add("act", lambda e: e.activation(CACT, VEC[:, V_CT:V_CT + 16], AF.Silu), r=["VEC"], w=["CACT"])
        psA = self.PS[:, 1, :]
        blk_i = 0
        for l in range(self.nlayers):
            for blk in range(3):
                wa = WA[blk_i % 2]
                wk = f"WA{blk_i % 2}"
                blk_i += 1
                for k in range(8):
                    add("pool", lambda e, wa=wa, k=k, l=l, blk=blk: e.dma_start(
                        out=wa[:, k, :], in_=self.w_ada[l, k * 128:(k + 1) * 128, blk * 3072:(blk + 1) * 3072]),
                        w=[f"{wk}.{k}"], dma=True)
                for j in range(24):
                    jj = blk * 24 + j
                    for k in range(8):
                        add("pe", lambda e, wa=wa, k=k, j=j, jj=jj: e.matmul(
                            psA[:, 2 * jj:2 * jj + 2], lhsT=wa[:, k, j * 128:(j + 1) * 128],
                            rhs=CACT[:, 2 * k:2 * k + 2], start=(k == 0), stop=(k == 7)),
                            r=[f"{wk}.{k}", "CACT"], w=["psA"])
            bada = VEC[:, V_BADA + 72 * l:V_BADA + 72 * (l + 1)]
            add("dve", lambda e, bada=bada: e.tensor_tensor(
                out=MOD, in0=psA[:, 0:144].rearrange("p (a b) -> p a b", b=2),
                in1=bada.rearrange("p (a b) -> p a b", b=1).to_broadcast([128, 72, 2]), op=ALU.add),
                r=["psA", "VEC"], w=["MOD"])
            norm_offs = [V_FFN1N, V_MIXN, V_FFN2N]
            coefs = [0.5, 1.0, 0.5]
            for i in range(3):
                nrm = VEC[:, norm_offs[i] + 8 * l:norm_offs[i] + 8 * (l + 1)]
                pa = self.PAR[:, ((l * 9 + 3 * i) * 16):((l * 9 + 3 * i) * 16) + 16].rearrange("p (a b) -> p a b", b=2)
                pb = self.PAR[:, ((l * 9 + 3 * i + 1) * 16):((l * 9 + 3 * i + 1) * 16) + 16].rearrange("p (a b) -> p a b", b=2)
                pg = self.PAR[:, ((l * 9 + 3 * i + 2) * 16):((l * 9 + 3 * i + 2) * 16) + 16].rearrange("p (a b) -> p a b", b=2)
                sh = MOD[:, (3 * i) * 8:(3 * i + 1) * 8, :]
                sc = MOD[:, (3 * i + 1) * 8:(3 * i + 2) * 8, :]
                gt = MOD[:, (3 * i + 2) * 8:(3 * i + 3) * 8, :]
                add("dve", lambda e, sc=sc: e.tensor_scalar(out=TMPM, in0=sc, scalar1=1.0, scalar2=None, op0=ALU.add),
                    r=["MOD"], w=["TMPM"])
                add("dve", lambda e, pa=pa, nrm=nrm: e.tensor_tensor(
                    out=pa, in0=TMPM, in1=nrm.rearrange("p (a b) -> p a b", b=1).to_broadcast([128, 8, 2]), op=ALU.mult),
                    r=["TMPM", "VEC"], w=["PAR"])
                add("dve", lambda e, pb=pb, sh=sh: e.tensor_copy(out=pb, in_=sh), r=["MOD"], w=["PAR"])
                add("dve", lambda e, pg=pg, gt=gt, cf=coefs[i]: e.tensor_scalar(
                    out=pg, in0=gt, scalar1=1.0, scalar2=cf, op0=ALU.add, op1=ALU.mult), r=["MOD"], w=["PAR"])
        P0, P1 = 64, 96
        ncols = self.ntiles * T
        for c0 in range(0, ncols, 2048):
            n = min(2048, ncols - c0)
            add("sp", lambda e, c0=c0, n=n: e.dma_start(out=POSI[P0:P1, 0:n],
                                                        in_=self.pos[0:1, c0:c0 + n].partition_broadcast(32)),
                w=["POSI"], dma=True)
            add("dve", lambda e, n=n: e.tensor_copy(out=ANG[P0:P1, 0:n], in_=POSI[P0:P1, 0:n]), r=["POSI"], w=["ANG"])
            add("dve", lambda e, n=n: e.tensor_scalar(out=ANG[P0:P1, 0:n], in0=ANG[P0:P1, 0:n],
                                                      scalar1=self.vcol(V_INVF, P0, P1), scalar2=None, op0=ALU.mult),
                r=["ANG", "VEC"], w=["ANG"])
            add("dve", lambda e, n=n: e.tensor_scalar(out=KF[P0:P1, 0:n], in0=ANG[P0:P1, 0:n],
                                                      scalar1=float(1.0 / TWO_PI), scalar2=MAGIC, op0=ALU.mult, op1=ALU.add),
                r=["ANG"], w=["KF"])
            add("dve", lambda e, n=n: e.tensor_scalar(out=KF[P0:P1, 0:n], in0=KF[P0:P1, 0:n],
                                                      scalar1=-MAGIC, scalar2=None, op0=ALU.add),
                r=["KF"], w=["KF"])
            add("dve", lambda e, n=n: e.scalar_tensor_tensor(out=RR[P0:P1, 0:n], in0=KF[P0:P1, 0:n], scalar=-C1,
                                                             in1=ANG[P0:P1, 0:n], op0=ALU.mult, op1=ALU.add),
                r=["KF", "ANG"], w=["RR"])
            add("dve", lambda e, n=n: e.scalar_tensor_tensor(out=RR[P0:P1, 0:n], in0=KF[P0:P1, 0:n], scalar=-C2,
                                                             in1=RR[P0:P1, 0:n], op0=ALU.mult, op1=ALU.add),
                r=["KF", "RR"], w=["RR"])
            add("dve", lambda e, n=n: e.tensor_scalar(out=RR[P0:P1, 0:n], in0=RR[P0:P1, 0:n],
                                                      scalar1=-PI_LO, scalar2=PI_LO, op0=ALU.max, op1=ALU.min),
                r=["RR"], w=["RR"])
            add("act", lambda e, n=n: e.activation(SS[P0:P1, 0:n], RR[P0:P1, 0:n], AF.Sin,
                                                   scale=self.vcol(V_SGN, P0, P1)),
                r=["RR", "VEC"], w=["SS"])
            add("act", lambda e, n=n: e.activation(CC[P0:P1, 0:n], RR[P0:P1, 0:n], AF.Sin, scale=0.5),
                r=["RR"], w=["CC"])
            add("dve", lambda e, n=n: e.tensor_tensor(out=CC[P0:P1, 0:n], in0=CC[P0:P1, 0:n], in1=CC[P0:P1, 0:n],
                                                      op=ALU.mult), r=["CC"], w=["CC"])
            add("dve", lambda e, n=n: e.tensor_scalar(out=CC[P0:P1, 0:n], in0=CC[P0:P1, 0:n],
                                                      scalar1=-2.0, scalar2=1.0, op0=ALU.mult, op1=ALU.add),
                r=["CC"], w=["CC"])
            add("sp", lambda e, c0=c0, n=n: e.dma_start(out=self.CT[:, c0:c0 + n], in_=CC[P0:P1, 0:n]),
                r=["CC"], w=["CTd"], dma=True)
            add("sp", lambda e, c0=c0, n=n: e.dma_start(out=self.ST[:, c0:c0 + n], in_=SS[P0:P1, 0:n]),
                r=["SS"], w=["STd"], dma=True)
        self.dump("PAR", self.PAR, (128, L * 9 * 16), F32, ["PAR"])
        self.dump("CACT", CACT, (128, 16), BF16, ["CACT"])
        A.off = base

    def norm_to_h(self, X, xk, H, l, sub, b, RS, SQ, TMP):
        add = self.add
        self.rms_stats([(X[:, k, :], [f"{xk}.{k}"]) for k in range(8)], D, RS, "RS", SQ)
        for k in range(8):
            tmp = TMP[:, k % 2, :]
            add("dve", lambda e, tmp=tmp, k=k: e.scalar_tensor_tensor(
                out=tmp, in0=X[:, k, :], scalar=self.par(l, 3 * sub, k, b), in1=RS, op0=ALU.mult, op1=ALU.mult),
                r=[f"{xk}.{k}", "RS", "PAR"], w=[f"TMP{k % 2}"])
            add("act", lambda e, tmp=tmp, k=k: e.activation(
                H[:, k, :], tmp, AF.Identity, bias=self.par(l, 3 * sub + 1, k, b)),
                r=[f"TMP{k % 2}", "PAR"], w=[f"H.{k}"])

    def f_pass(self, l, which, src, dst, final=False):
        A = self.arena
        base = A.off
        add = self.add
        WGU = A.alloc(BF16, (8, 2 * DFF))
        WD = A.alloc(BF16, (NF, D))
        XB = [A.alloc(F32, (8, T)) for _ in range(2)]
        H = A.alloc(BF16, (8, T))
        ACTB = A.alloc(BF16, (NF, T))
        SQ = A.alloc(BF16, (2, T))
        RS = A.alloc(F32, (T,))
        TMP = A.alloc(F32, (2, T))
        sub = 0 if which == 0 else 2
        wgu = self.wgu[which]
        wd = self.wd[which]
        wguv = wgu[l].rearrange("(k p) n -> p k n", p=128)
        for f0 in range(0, NF, 6):
            f1 = min(NF, f0 + 6)
            for half in range(2):
                c0 = half * DFF + f0 * 128
                c1 = half * DFF + f1 * 128
                add("pool", lambda e, c0=c0, c1=c1: e.dma_start(out=WGU[:, :, c0:c1], in_=wguv[:, :, c0:c1]),
                    w=[f"WGU{half}.{f}" for f in range(f0, f1)], dma=True)
        for f0 in range(0, NF, 2):
            add("pool", lambda e, f0=f0: e.dma_start(
                out=WD[:, f0:f0 + 2, :],
                in_=wd[l, f0 * 128:(f0 + 2) * 128, :].rearrange("(f p) d -> p f d", p=128)),
                w=[f"WD.{f0}", f"WD.{f0 + 1}"], dma=True)
        srcv = src.rearrange("(k p) t -> p k t", p=128)
        dstv = dst.rearrange("(k p) t -> p k t", p=128)
        PS = self.PS

        def load(t):
            xb = t % 2
            add("sp", lambda e: e.dma_start(out=XB[xb], in_=srcv[:, :, t * T:(t + 1) * T]),
                r=[f"XD.{t}"], w=[f"X{xb}.{k}" for k in range(8)], dma=True)

        load(0)

        def do_tile(t):
            xb = t % 2
            X = XB[xb]
            xk = f"X{xb}"
            b = t // NTS
            if t + 1 < self.ntiles:
                load(t + 1)
            if t == 0:
                self.norm_to_h(X, xk, H, l, sub, b, RS, SQ, TMP)
            for f in range(NF):
                g = f % 2
                for k in range(8):
                    add("pe", lambda e, f=f, k=k, g=g: e.matmul(
                        PS[:, 1 + g, :], lhsT=WGU[:, k, f * 128:(f + 1) * 128], rhs=H[:, k, :],
                        start=(k == 0), stop=(k == 7)), r=[f"WGU0.{f}", f"H.{k}"], w=[f"psG{g}"])
                for k in range(8):
                    add("pe", lambda e, f=f, k=k, g=g: e.matmul(
                        PS[:, 3 + g, :], lhsT=WGU[:, k, DFF + f * 128:DFF + (f + 1) * 128], rhs=H[:, k, :],
                        start=(k == 0), stop=(k == 7)), r=[f"WGU1.{f}", f"H.{k}"], w=[f"psU{g}"])
                add("act", lambda e, g=g: e.activation(TMP[:, g, :], PS[:, 1 + g, :], AF.Silu),
                    r=[f"psG{g}"], w=[f"TMP{g}"])
                add("dve", lambda e, f=f, g=g: e.tensor_tensor(out=ACTB[:, f, :], in0=TMP[:, g, :], in1=PS[:, 3 + g, :],
                                                               op=ALU.mult),
                    r=[f"TMP{g}", f"psU{g}"], w=[f"ACTB.{f}"])
            if t + 1 < self.ntiles:
                self.norm_to_h(XB[(t + 1) % 2], f"X{(t + 1) % 2}", H, l, sub, (t + 1) // NTS, RS, SQ, TMP)
            for d in range(8):
                yb = d % 2
                for f in range(NF):
                    add("pe", lambda e, f=f, d=d, yb=yb: e.matmul(
                        PS[:, 5 + yb, :], lhsT=WD[:, f, d * 128:(d + 1) * 128], rhs=ACTB[:, f, :],
                        start=(f == 0), stop=(f == NF - 1)), r=[f"WD.{f}", f"ACTB.{f}"], w=[f"psY{yb}"])
                add("dve", lambda e, d=d, yb=yb: e.scalar_tensor_tensor(
                    out=X[:, d, :], in0=PS[:, 5 + yb, :], scalar=self.par(l, 3 * sub + 2, d, b), in1=X[:, d, :],
                    op0=ALU.mult, op1=ALU.add), r=[f"psY{yb}", f"{xk}.{d}", "PAR"], w=[f"{xk}.{d}"])
            if final:
                self.rms_stats([(X[:, k, :], [f"{xk}.{k}"]) for k in range(8)], D, RS, "RS", SQ)
                for k in range(8):
                    add("dve", lambda e, k=k: e.scalar_tensor_tensor(
                        out=X[:, k, :], in0=X[:, k, :], scalar=self.vcol(V_FINALN + k), in1=RS,
                        op0=ALU.mult, op1=ALU.mult), r=[f"{xk}.{k}", "RS", "VEC"], w=[f"{xk}.{k}"])
            o = add("sp", lambda e, t=t, X=X: e.dma_start(out=dstv[:, :, t * T:(t + 1) * T], in_=X),
                    r=[f"{xk}.{k}" for k in range(8)], w=[f"XD.{t}"], dma=True)
            if final:
                self.out_dmas.append(o)

        for t in range(self.ntiles):
            do_tile(t)
        A.off = base

    def m1_pass(self, l):
        A = self.arena
        base = A.off
        add = self.add
        PS = self.PS
        WIN = A.alloc(BF16, (8, WIN_COLS))
        WUQ = A.alloc(BF16, (3, WUQ_COLS))
        WUKV = A.alloc(BF16, (2, 1024))
        XB = [A.alloc(F32, (8, T)) for _ in range(2)]
        H = A.alloc(BF16, (8, T))
        CQN = A.alloc(BF16, (3, T))
        CKVN = A.alloc(BF16, (2, T))
        CS = A.alloc(F32, (2, T))
        SQ = A.alloc(BF16, (2, T))
        RS = A.alloc(F32, (T,))
        RS2 = A.alloc(F32, (T,))
        TMP = A.alloc(F32, (2, T))
        U = A.alloc(F32, (4, T + 2))
        GB = A.alloc(F32, (4, T))
        CV = A.alloc(F32, (4, T))
        YC = A.alloc(F32, (2, T))
        CG = A.alloc(BF16, (4, T))
        T1B = A.alloc(F32, (2, T))
        T2B = A.alloc(F32, (2, T))
        KR = A.alloc(BF16, (T,))
        QS = A.alloc(BF16, (NH, T))
        KS = A.alloc(BF16, (NH, T))
        VS = A.alloc(BF16, (4, 4, 192))
        for k in range(8):
            add("pool", lambda e, k=k: e.dma_start(out=WIN[:, k, :], in_=self.w_in[l, k * 128:(k + 1) * 128, :]),
                w=[f"WIN.{k}"], dma=True)
        add("pool", lambda e: e.dma_start(out=WUQ, in_=self.w_uq[l].rearrange("(k p) n -> p k n", p=128)),
            w=["WUQ"], dma=True)
        add("pool", lambda e: e.dma_start(out=WUKV, in_=self.w_ukv[l].rearrange("(k p) n -> p k n", p=128)),
            w=["WUKV"], dma=True)
        add("dve", lambda e: e.memset(VS, 1.0), w=["VS"])
        srcv = self.XW.rearrange("(k p) t -> p k t", p=128)

        def load(t):
            xb = t % 2
            add("sp", lambda e: e.dma_start(out=XB[xb], in_=srcv[:, :, t * T:(t + 1) * T]),
                r=[f"XD.{t}"], w=[f"X{xb}.{k}" for k in range(8)], dma=True)

        def proj(col0, ncol, bank, tag):
            for k in range(8):
                add("pe", lambda e, k=k: e.matmul(PS[0:ncol, bank, :], lhsT=WIN[:, k, col0:col0 + ncol], rhs=H[:, k, :],
                                                  start=(k == 0), stop=(k == 7)),
                    r=[f"WIN.{k}", f"H.{k}"], w=[f"ps{bank}"])

        load(0)

        def do_tile(t):
            xb = t % 2
            X = XB[xb]
            xk = f"X{xb}"
            b = t // NTS
            j = t % NTS
            if t + 1 < self.ntiles:
                load(t + 1)
            add("sp", lambda e, t=t: e.dma_start(out=CS[64:96, 0, :], in_=self.CT[:, t * T:(t + 1) * T]),
                r=["CTd"], w=["CS0"], dma=True)
            add("sp", lambda e, t=t: e.dma_start(out=CS[64:96, 1, :], in_=self.ST[:, t * T:(t + 1) * T]),
                r=["STd"], w=["CS1"], dma=True)
            if t == 0:
                self.norm_to_h(X, xk, H, l, 1, b, RS, SQ, TMP)
            for (col0, nch, DST, dk, gcol, nfeat) in ((0, 3, CQN, "CQN", V_QAN + 3 * l, 384),
                                                       (384, 2, CKVN, "CKVN", V_KVAN + 2 * l, 256)):
                banks = []
                for m in range(nch):
                    bk = self.bank()
                    banks.append(bk)
                    proj(col0 + m * 128, 128, bk, dk)
                self.rms_stats([(PS[:, bk, :], [f"ps{bk}"]) for bk in banks], nfeat, RS, "RS", SQ)
                for m, bk in enumerate(banks):
                    add("dve", lambda e, m=m, bk=bk, DST=DST, gcol=gcol: e.scalar_tensor_tensor(
                        out=DST[:, m, :], in0=PS[:, bk, :], scalar=self.vcol(gcol + m), in1=RS,
                        op0=ALU.mult, op1=ALU.mult), r=[f"ps{bk}", "RS", "VEC"], w=[f"{dk}.{m}"])
            bk1 = self.bank()
            proj(2176, 96, bk1, "kpe")
            bk2 = self.bank()
            proj(2272, 96, bk2, "kpesw")
            add("dve", lambda e, bk1=bk1: e.tensor_tensor(out=T1B[64:96, 0, :], in0=PS[64:96, bk1, :], in1=CS[64:96, 0, :],
                                                           op=ALU.mult), r=[f"ps{bk1}", "CS0"], w=["T1.0"])
            add("dve", lambda e, bk2=bk2: e.tensor_tensor(out=T2B[64:96, 0, :], in0=PS[64:96, bk2, :], in1=CS[64:96, 1, :],
                                                           op=ALU.mult), r=[f"ps{bk2}", "CS1"], w=["T2.0"])
            add("dve", lambda e: e.tensor_tensor(out=KR[64:96, :], in0=T1B[64:96, 0, :], in1=T2B[64:96, 0, :], op=ALU.add),
                r=["T1.0", "T2.0"], w=["KR"])
            add("sp", lambda e, t=t: e.dma_start(
                out=self.KD[t].rearrange("p (a b) -> p a b", a=NH)[64:96],
                in_=KR[64:96, None, :].to_broadcast([32, NH, T])),
                r=["KR"], w=[f"KDr.{t}"], dma=True)
            if j == 0:
                add("pool", lambda e: e.memset(U[:, :, 0:2], 0.0), w=[f"U.{c}" for c in range(4)])
            for c in range(4):
                bgc = self.bank()
                proj(640 + 512 + c * 128, 128, bgc, "gc")
                bvl = self.bank()
                proj(640 + 1024 + c * 128, 128, bvl, "val")
                bgb = self.bank()
                proj(640 + c * 128, 128, bgb, "gb")
                add("act", lambda e, bgc=bgc: e.activation(TMP[:, 0, :], PS[:, bgc, :], AF.Copy),
                    r=[f"ps{bgc}"], w=["TMP0"])
                add("dve", lambda e, c=c, bvl=bvl: e.tensor_tensor(out=U[:, c, 2:T + 2], in0=TMP[:, 0, :],
                                                                   in1=PS[:, bvl, :], op=ALU.mult),
                    r=["TMP0", f"ps{bvl}"], w=[f"U.{c}"])
                add("act", lambda e, c=c, bgb=bgb: e.activation(GB[:, c, :], PS[:, bgb, :], AF.Copy),
                    r=[f"ps{bgb}"], w=[f"GB.{c}"])
                yc = YC[:, c % 2, :]
                wc = V_CONVW + 12 * l
                add("pool", lambda e, c=c, yc=yc, wc=wc: e.tensor_scalar(
                    out=yc, in0=U[:, c, 2:T + 2], scalar1=self.vcol(wc + 8 + c), scalar2=0.0, op0=ALU.mult, op1=ALU.add),
                    r=[f"U.{c}", "VEC"], w=[f"YC{c % 2}"])
                add("dve", lambda e, c=c, yc=yc, wc=wc: e.scalar_tensor_tensor(
                    out=yc, in0=U[:, c, 1:T + 1], scalar=self.vcol(wc + 4 + c), in1=yc, op0=ALU.mult, op1=ALU.add),
                    r=[f"U.{c}", "VEC", f"YC{c % 2}"], w=[f"YC{c % 2}"])
                add("dve", lambda e, c=c, yc=yc, wc=wc: e.scalar_tensor_tensor(
                    out=yc, in0=U[:, c, 0:T], scalar=self.vcol(wc + c), in1=yc, op0=ALU.mult, op1=ALU.add),
                    r=[f"U.{c}", "VEC", f"YC{c % 2}"], w=[f"YC{c % 2}"])
                add("pool", lambda e, c=c, yc=yc: e.tensor_tensor(out=CV[:, c, :], in0=yc, in1=GB[:, c, :], op=ALU.mult),
                    r=[f"YC{c % 2}", f"GB.{c}"], w=[f"CV.{c}"])
                add("pool", lambda e, c=c: e.tensor_copy(out=U[:, c, 0:2], in_=U[:, c, T:T + 2]),
                    r=[f"U.{c}"], w=[f"U.{c}"])
            if t + 1 < self.ntiles:
                self.norm_to_h(XB[(t + 1) % 2], f"X{(t + 1) % 2}", H, l, 1, (t + 1) // NTS, RS, SQ, TMP)
            for h in range(NH):
                ba = self.bank()
                for k in range(3):
                    add("pe", lambda e, k=k, h=h, ba=ba: e.matmul(
                        PS[0:96, ba, :], lhsT=WUQ[:, k, h * 192:h * 192 + 96], rhs=CQN[:, k, :],
                        start=(k == 0), stop=(k == 2)), r=["WUQ", f"CQN.{k}"], w=[f"ps{ba}"])
                bb = self.bank()
                for k in range(3):
                    add("pe", lambda e, k=k, h=h, bb=bb: e.matmul(
                        PS[0:96, bb, :], lhsT=WUQ[:, k, h * 192 + 96:h * 192 + 192], rhs=CQN[:, k, :],
                        start=(k == 0), stop=(k == 2)), r=["WUQ", f"CQN.{k}"], w=[f"ps{bb}"])
                add("act", lambda e, h=h, ba=ba: e.activation(QS[0:64, h, :], PS[0:64, ba, :], AF.Copy),
                    r=[f"ps{ba}"], w=["QS"])
                tb = h % 2
                add("dve", lambda e, ba=ba, tb=tb: e.tensor_tensor(out=T1B[64:96, tb, :], in0=PS[64:96, ba, :],
                                                                    in1=CS[64:96, 0, :], op=ALU.mult),
                    r=[f"ps{ba}", "CS0"], w=[f"T1.{tb}"])
                add("dve", lambda e, bb=bb, tb=tb: e.tensor_tensor(out=T2B[64:96, tb, :], in0=PS[64:96, bb, :],
                                                                    in1=CS[64:96, 1, :], op=ALU.mult),
                    r=[f"ps{bb}", "CS1"], w=[f"T2.{tb}"])
                add("pool", lambda e, h=h, tb=tb: e.tensor_tensor(out=QS[64:96, h, :], in0=T1B[64:96, tb, :],
                                                                   in1=T2B[64:96, tb, :], op=ALU.add),
                    r=[f"T1.{tb}", f"T2.{tb}"], w=["QS"])
            for h in range(NH):
                bk = self.bank()
                for k in range(2):
                    add("pe", lambda e, k=k, h=h, bk=bk: e.matmul(
                        PS[0:64, bk, :], lhsT=WUKV[:, k, h * 64:(h + 1) * 64], rhs=CKVN[:, k, :],
                        start=(k == 0), stop=(k == 1)), r=["WUKV", f"CKVN.{k}"], w=[f"ps{bk}"])
                add("act", lambda e, h=h, bk=bk: e.activation(KS[0:64, h, :], PS[0:64, bk, :], AF.Copy),
                    r=[f"ps{bk}"], w=["KSn"])
            for c4 in range(4):
                bk = self.bank()
                for k in range(2):
                    add("pe", lambda e, k=k, c4=c4, bk=bk: e.matmul(
                        PS[:, bk, :], lhsT=CKVN[:, k, c4 * 128:(c4 + 1) * 128], rhs=WUKV[:, k, 512:1024],
                        start=(k == 0), stop=(k == 1)), r=["WUKV", f"CKVN.{k}"], w=[f"ps{bk}"])
                pv4 = PS[:, bk, :].rearrange("p (a s d) -> p a s d", a=4, s=2)
                add("dve", lambda e, c4=c4, pv4=pv4: e.tensor_copy(out=VS[:, c4, :, 0:64], in_=pv4[:, :, 0, :]),
                    r=[f"ps{bk}"], w=["VS"])
                add("act", lambda e, c4=c4, pv4=pv4: e.activation(VS[:, c4, :, 128:192], pv4[:, :, 1, :], AF.Copy),
                    r=[f"ps{bk}"], w=["VS"])
            self.rms_stats([(CV[:, c, :], [f"CV.{c}"]) for c in range(4)], 512, RS2, "RS2", SQ)
            for c in range(4):
                add("dve", lambda e, c=c: e.scalar_tensor_tensor(
                    out=CG[:, c, :], in0=CV[:, c, :], scalar=self.vcol(V_CON + 4 * l + c), in1=RS2,
                    op0=ALU.mult, op1=ALU.mult), r=[f"CV.{c}", "VEC", "RS2"], w=["CG"])
            add("sp", lambda e, t=t: e.dma_start(out=self.CD[t].rearrange("p (a b) -> p a b", a=4), in_=CG),
                r=["CG"], w=[f"CDd.{t}"], dma=True)
            add("sp", lambda e, t=t: e.dma_start(out=self.QD[t].rearrange("p (a b) -> p a b", a=NH), in_=QS[0:96]),
                r=["QS"], w=[f"QDd.{t}"], dma=True)
            add("sp", lambda e, t=t: e.dma_start(out=self.KD[t].rearrange("p (a b) -> p a b", a=NH)[0:64], in_=KS[0:64]),
                r=["KSn"], w=[f"KDd.{t}"], dma=True)
            add("sp", lambda e, t=t: e.dma_start(out=self.VD[t].rearrange("p (a b c) -> p a b c", a=4, b=4), in_=VS),
                r=["VS"], w=[f"VDd.{t}"], dma=True)

        for t in range(self.ntiles):
            do_tile(t)
        A.off = base

    def m2_pass(self, l):
        A = self.arena
        base = A.off
        add = self.add
        PS = self.PS
        KT = A.alloc(BF16, (NH, SEQ))
        VT = A.alloc(BF16, (32, 768))
        WO = A.alloc(BF16, (8, D))
        QTB = [A.alloc(BF16, (NH, T)) for _ in range(2)]
        CGB = [A.alloc(BF16, (4, T)) for _ in range(2)]
        X = A.alloc(F32, (8, T))
        ANF = A.alloc(F32, (4, T))
        AG = A.alloc(BF16, (4, T))
        PT = A.alloc(BF16, (4, T))
        RBS = A.alloc(F32, (T,))
        SQ = A.alloc(BF16, (4, T))
        RSA = A.alloc(F32, (T,))
        add("pool", lambda e: e.dma_start(out=WO, in_=self.w_o[l].rearrange("(k p) n -> p k n", p=128)),
            w=["WO"], dma=True)
        xv = self.XW.rearrange("(k p) t -> p k t", p=128)
        ST_B = (0, 1, 4)
        OT_B = (2, 3)
        SSA_B = 5
        Y_B = (6, 6)
        DUM_B = 7
        NQA = 2

        def load(t):
            qb = t % 2
            add("sp", lambda e: e.dma_start(out=QTB[qb][0:96], in_=self.QD[t].rearrange("p (a b) -> p a b", a=NH)),
                r=[f"QDd.{t}"], w=[f"QT{qb}"], dma=True)
            add("sp", lambda e: e.dma_start(out=CGB[qb], in_=self.CD[t].rearrange("p (a b) -> p a b", a=4)),
                r=[f"CDd.{t}"], w=[f"CGB{qb}"], dma=True)

        def load_kv(t):
            j = t % NTS
            add("sp", lambda e: e.dma_start(out=KT[0:96, :, j * T:(j + 1) * T],
                                            in_=self.KD[t].rearrange("p (a b) -> p a b", a=NH)),
                r=[f"KDd.{t}", f"KDr.{t}"], w=[f"KT.{j}"], dma=True)
            add("sp", lambda e: e.dma_start(out=VT[:, 4 * j:4 * j + 4, :],
                                            in_=self.VD[t].rearrange("p (a b) -> p a b", a=4)),
                r=[f"VDd.{t}"], w=[f"VT.{j}"], dma=True)

        carry = []

        def tile_end(t, qb, b):
            def part0():
                for pr in range(4):
                    add("pe", lambda e, pr=pr: e.matmul(PS[:, SSA_B, :], lhsT=self.ONESB, rhs=SQ[:, pr, :],
                                                        start=(pr == 0), stop=(pr == 3)),
                        r=[f"SQ{pr}", "ONESB"], w=["psSSA"])
                self.rstd_from_psum(PS[:, SSA_B, :], "psSSA", 512, RSA, "RSA")
                for pr in range(4):
                    add("dve", lambda e, pr=pr: e.scalar_tensor_tensor(
                        out=AG[:, pr, :], in0=ANF[:, pr, :], scalar=self.vcol(V_AON + 8 * l + pr), in1=RSA,
                        op0=ALU.mult, op1=ALU.mult), r=[f"ANF.{pr}", "RSA", "VEC"], w=[f"AG.{pr}"])
            carry.append(part0)

            def wo_mm(d, k):
                yb = Y_B[d % 2]
                rhs = AG[:, k, :] if k < 4 else CGB[qb][:, k - 4, :]
                rk = f"AG.{k}" if k < 4 else f"CGB{qb}"
                add("pe", lambda e: e.matmul(PS[:, yb, :], lhsT=WO[:, k, d * 128:(d + 1) * 128], rhs=rhs,
                                             start=(k == 0), stop=(k == 7)), r=["WO", rk], w=[f"psY{yb}"])
                if k == 7:
                    add("dve", lambda e: e.scalar_tensor_tensor(
                        out=X[:, d, :], in0=PS[:, yb, :], scalar=self.par(l, 5, d, b), in1=X[:, d, :],
                        op0=ALU.mult, op1=ALU.add), r=[f"psY{yb}", f"X.{d}", "PAR"], w=[f"X.{d}"])
            for d in range(8):
                for k in range(8):
                    carry.append(lambda d=d, k=k: wo_mm(d, k))

            def fin():
                add("sp", lambda e: e.dma_start(out=xv[:, :, t * T:(t + 1) * T], in_=X),
                    r=[f"X.{k}" for k in range(8)], w=[f"XD.{t}"], dma=True)
            carry.append(fin)

        def load_x(t):
            add("sp", lambda e: e.dma_start(out=X, in_=xv[:, :, t * T:(t + 1) * T]),
                r=[f"XD.{t}"], w=[f"X.{k}" for k in range(8)], dma=True)

        load(0)
        load_x(0)

        def do_tile(t):
            qb = t % 2
            QT = QTB[qb]
            b = t // NTS
            j = t % NTS
            load_kv(t)
            nch = 4 * j + 4
            blocks = [(h, c) for h in range(NH) for c in range(nch)]
            nb = len(blocks)
            quota = -(-66 // (nb - 8))
            if t == 0 and t + 1 < self.ntiles:
                load(t + 1)

            def qk(i):
                h, c = blocks[i]
                lo = max(0, c - 4 * j) * 128
                sb = ST_B[i % 3]
                diag = c >= 4 * j
                add("pe", lambda e: e.matmul(PS[:, sb, lo:T], lhsT=KT[0:96, h, c * 128:(c + 1) * 128],
                                             rhs=QT[0:96, h, lo:T], start=True, stop=not diag),
                    r=[f"KT.{c // 4}", f"QT{qb}"], w=[f"psST{i % 3}"])
                if diag:
                    add("pe", lambda e: e.matmul(PS[:, sb, lo:lo + 128], lhsT=self.IDN, rhs=self.MB,
                                                 start=False, stop=True),
                        r=["IDN", "MB"], w=[f"psST{i % 3}"])
                add("act", lambda e: e.activation(PT[:, i % 4, lo:T], PS[:, sb, lo:T], AF.Exp, scale=SM_SCALE),
                    r=[f"psST{i % 3}"], w=[f"PT{i % 4}"])

            def head_end(h):
                ob = OT_B[h % 2]
                pr = h // 2
                if h % 2 == 0:
                    o_lo, o_hi, s_lo, s_hi = 0, 64, 64, 128
                else:
                    o_lo, o_hi, s_lo, s_hi = 64, 128, 0, 64
                add("dve", lambda e: e.reciprocal(RBS[o_lo:o_hi, :], PS[s_lo:s_hi, ob, :]),
                    r=[f"psOT{h % 2}"], w=["RBS"])
                add("dve", lambda e: e.tensor_tensor(out=ANF[o_lo:o_hi, pr, :], in0=PS[o_lo:o_hi, ob, :],
                                                     in1=RBS[o_lo:o_hi, :], op=ALU.mult),
                    r=[f"psOT{h % 2}", "RBS"], w=[f"ANF.{pr}"])
                add("pool", lambda e: e.tensor_tensor(out=SQ[o_lo:o_hi, pr, :], in0=ANF[o_lo:o_hi, pr, :],
                                                      in1=ANF[o_lo:o_hi, pr, :], op=ALU.mult),
                    r=[f"ANF.{pr}"], w=[f"SQ{pr}"])

            pend = []

            def pv(i):
                h, c = blocks[i]
                lo = max(0, c - 4 * j) * 128
                ob = OT_B[h % 2]
                pr = h // 2
                c0 = 0 if h % 2 == 0 else 64
                add("pe", lambda e: e.matmul(PS[:, ob, lo:T], lhsT=VT[:, c, pr * 192 + c0:pr * 192 + c0 + 128],
                                             rhs=PT[:, i % 4, lo:T], start=(c == 0), stop=(c == nch - 1)),
                    r=[f"PT{i % 4}", f"VT.{c // 4}"], w=[f"psOT{h % 2}"])
                if c == nch - 1:
                    pend.append([3, h])

            def tick():
                for p in list(pend):
                    p[0] -= 1
                    if p[0] <= 0:
                        pend.remove(p)
                        head_end(p[1])

            for i in range(min(NQA, nb)):
                qk(i)
            for i in range(nb):
                if i + NQA < nb:
                    qk(i + NQA)
                pv(i)
                tick()
                did_filler = False
                if i >= 3 and carry:
                    nq = 1 if i == 3 else quota
                    for _ in range(nq):
                        if carry:
                            carry.pop(0)()
                            did_filler = True
                    if not carry:
                        load_x(t)
                        if t + 1 < self.ntiles:
                            load(t + 1)
                if FILLER and not did_filler:
                    add("pe", lambda e: e.matmul(PS[:, DUM_B, 0:DUMN], lhsT=self.IDN, rhs=PT[:, 0, 0:DUMN],
                                                 start=True, stop=True), r=["IDN"], w=["psDUM"])
            while pend:
                tick()
            tile_end(t, qb, b)

        for t in range(self.ntiles):
            do_tile(t)
        while carry:
            carry.pop(0)()
        A.off = base

    def build(self):
        nc = bass.Bass("TRN2", target_bir_lowering=False)
        self.nc = nc
        self.declare_dram(nc)
        self.out_dmas = []
        with nc.sbuf_tensor("arena", [128, ARENA_WORDS], F32) as arena_t, \
                nc.psum_tensor("ps", [128, 8, 512], F32) as ps_t:
            self.arena = Arena(arena_t, ARENA_WORDS)
            self.PS = ps_t
            A = self.arena
            self.VEC = A.alloc(F32, (NV,))
            self.PAR = A.alloc(F32, (L * 9 * 16,))
            self.ONESB = A.alloc(BF16, (128,))
            self.IDN = A.alloc(BF16, (128,))
            self.MB = A.alloc(BF16, (128,))
            self.ONESF = A.alloc(F32, (128,))
            self.SCR1 = A.alloc(F32, (2,))
            self.prep()
            self.barrier()
            stop = self.stop
            done = False
            for l in range(self.nlayers):
                src = self.xT if l == 0 else self.XW
                for pname in ("F1", "M1", "M2", "F2"):
                    if pname == "F1":
                        self.f_pass(l, 0, src, self.XW)
                    elif pname == "M1":
                        self.m1_pass(l)
                    elif pname == "M2":
                        self.m2_pass(l)
                    else:
                        last = (l == self.nlayers - 1)
                        self.f_pass(l, 1, self.XW, self.yT if last else self.XW, final=last)
                    self.barrier()
                    if stop is not None and stop == (l, pname):
                        done = True
                        break
                if done:
                    break
            self.S.add("sp", None, r=(), w=["ARENA"])
            self.S.finalize()
            with nc.semaphore("s_pe") as s_pe, nc.semaphore("s_act") as s_act, nc.semaphore("s_dve") as s_dve, \
                    nc.semaphore("s_pool") as s_pool, nc.semaphore("s_sp") as s_sp:
                sems = {"pe": s_pe, "act": s_act, "dve": s_dve, "pool": s_pool, "sp": s_sp}
                import contextlib
                with contextlib.ExitStack() as es:
                    dsems = {"sp": [es.enter_context(nc.semaphore(f"d_sp{i}")) for i in range(Sched.NSEM)],
                             "pool": [es.enter_context(nc.semaphore(f"d_pl{i}")) for i in range(Sched.NSEM)]}
                    with nc.Block() as block:
                        @block.tensor
                        def _(e):
                            self.S.emit_engine("pe", e, sems, dsems)

                        @block.scalar
                        def _(e):
                            self.S.emit_engine("act", e, sems, dsems)

                        @block.vector
                        def _(e):
                            self.S.emit_engine("dve", e, sems, dsems)

                        @block.gpsimd
                        def _(e):
                            self.S.emit_engine("pool", e, sems, dsems)

                        @block.sync
                        def _(e):
                            self.S.emit_engine("sp", e, sems, dsems)
        return nc


def _chunks(v, p=128):
    v = np.asarray(v, np.float32)
    return np.ascontiguousarray(v.reshape(-1, p).T)


def make_shared(inputs):
    f32 = np.float32
    w_in = np.asarray(inputs["w_in"], f32)
    cq, ckv, kpe = w_in[:, :, 0:384], w_in[:, :, 384:640], w_in[:, :, 640:672]
    gb, gc, val = w_in[:, :, 672:1184], w_in[:, :, 1184:1696], w_in[:, :, 1696:2208]
    z64 = np.zeros((L, D, 64), f32)
    kpe_sw = np.concatenate([kpe[:, :, 16:32], kpe[:, :, 0:16]], axis=-1)
    w_in_p = np.ascontiguousarray(np.concatenate([cq, ckv, gb, gc, val, z64, kpe, z64, kpe_sw], axis=-1))
    assert w_in_p.shape[-1] == WIN_COLS
    w_uq = np.asarray(inputs["w_uq"], f32).reshape(L, 384, NH, 96)
    nope, pe = w_uq[..., 0:64], w_uq[..., 64:96]
    pe_sw = np.concatenate([pe[..., 16:32], pe[..., 0:16]], axis=-1)
    zq = np.zeros((L, 384, NH, 64), f32)
    w_uq_p = np.ascontiguousarray(np.concatenate([nope, pe, zq, pe_sw], axis=-1).reshape(L, 384, WUQ_COLS))
    w_ukv = np.asarray(inputs["w_ukv"], f32).reshape(L, 256, NH, 128)
    w_ukv_p = np.ascontiguousarray(np.concatenate([w_ukv[..., 0:64].reshape(L, 256, 512),
                                                   w_ukv[..., 64:128].reshape(L, 256, 512)], axis=-1))
    consts = np.zeros((128, 384), f32)
    consts[:, 0:128] = 1.0
    consts[:, 128:256] = np.eye(128, dtype=f32)
    consts[:, 256:384] = np.where(np.arange(128)[:, None] > np.arange(128)[None, :], -30000.0, 0.0)
    shared = {
        "w_ada": np.ascontiguousarray(inputs["w_ada"], f32),
        "wgu1": np.ascontiguousarray(inputs["ffn1_w_gu"], f32), "wd1": np.ascontiguousarray(inputs["ffn1_w_down"], f32),
        "wgu2": np.ascontiguousarray(inputs["ffn2_w_gu"], f32), "wd2": np.ascontiguousarray(inputs["ffn2_w_down"], f32),
        "w_in_p": w_in_p, "w_uq_p": w_uq_p, "w_ukv_p": w_ukv_p,
        "w_o": np.ascontiguousarray(inputs["w_o"], f32), "consts": consts,
    }
    vec = np.zeros((128, NV), f32)
    for l in range(L):
        vec[:, V_BADA + 72 * l:V_BADA + 72 * (l + 1)] = _chunks(inputs["b_ada"][l])
        vec[:, V_FFN1N + 8 * l:V_FFN1N + 8 * (l + 1)] = _chunks(inputs["ffn1_norm"][l])
        vec[:, V_MIXN + 8 * l:V_MIXN + 8 * (l + 1)] = _chunks(inputs["mix_norm"][l])
        vec[:, V_FFN2N + 8 * l:V_FFN2N + 8 * (l + 1)] = _chunks(inputs["ffn2_norm"][l])
        vec[:, V_QAN + 3 * l:V_QAN + 3 * (l + 1)] = _chunks(inputs["q_a_norm"][l])
        vec[:, V_KVAN + 2 * l:V_KVAN + 2 * (l + 1)] = _chunks(inputs["kv_a_norm"][l])
        cw = np.asarray(inputs["conv_w"][l], f32)
        for jt in range(3):
            vec[:, V_CONVW + 12 * l + 4 * jt:V_CONVW + 12 * l + 4 * (jt + 1)] = _chunks(cw[jt])
        vec[:, V_AON + 8 * l:V_AON + 8 * l + 4] = _chunks(inputs["attn_out_norm"][l])
        vec[:, V_CON + 4 * l:V_CON + 4 * (l + 1)] = _chunks(inputs["conv_out_norm"][l])
    vec[:, V_FINALN:V_FINALN + 8] = _chunks(inputs["final_norm"])
    inv_freq = (1.0 / (np.float32(10000.0) ** (np.arange(0, 32, 2, dtype=f32) / np.float32(32)))).astype(f32)
    p = np.arange(128)
    vec[:, V_INVF] = inv_freq[p % 16]
    vec[:, V_SGN] = np.where((p % 32) < 16, -1.0, 1.0)
    vec[:, V_HALFPI] = np.float32(np.pi / 2)
    vec[:, V_EPS] = np.float32(EPS)
    return shared, vec


def make_core_inputs(inputs, shared, vec, i):
    f32 = np.float32
    x = np.asarray(inputs["x"][2 * i:2 * i + 2], f32)
    xT = np.ascontiguousarray(x.transpose(2, 0, 1).reshape(D, TOK))
    pos = np.ascontiguousarray(np.asarray(inputs["positions"][2 * i:2 * i + 2], np.int32).reshape(1, TOK))
    v = vec.copy()
    c = np.asarray(inputs["c"][2 * i:2 * i + 2], f32)
    v[:, V_CT:V_CT + 16] = c.reshape(2, 8, 128).transpose(2, 1, 0).reshape(128, 16)
    m = dict(shared)
    m["xT"] = xT
    m["pos"] = pos
    m["vecs"] = v
    return m


_NC_CACHE = {}


def kernel(**inputs):
    inputs = {k: np.asarray(v) for k, v in inputs.items()}
    shared, vec = make_shared(inputs)
    in_maps = [make_core_inputs(inputs, shared, vec, i) for i in range(NCORES)]
    if "nc" not in _NC_CACHE:
        _NC_CACHE["nc"] = Builder({}).build()
    nc = _NC_CACHE["nc"]
    res = run_bass_kernel_spmd(nc, in_maps, core_ids=list(range(NCORES)))
    out = np.empty((2 * NCORES, SEQ, D), np.float32)
    for i in range(NCORES):
        yT = np.asarray(res.results[i]["yT"], np.float32)
        out[2 * i:2 * i + 2] = yT.reshape(D, 2, SEQ).transpose(1, 2, 0)
    return out
```

```python
import numpy as np
import concourse.bass as bass
import concourse.mybir as mybir
from concourse.bass_utils import run_bass_kernel_spmd

F32 = mybir.dt.float32
BF16 = mybir.dt.bfloat16
I32 = mybir.dt.int32
AF = mybir.ActivationFunctionType
ALU = mybir.AluOpType

NCORES = 8
D = 1024
SEQ = 4096
NSEQ = 2
TOK = NSEQ * SEQ
T = 512
NTS = SEQ // T
NT = NSEQ * NTS
DFF = 2816
NF = 22
L = 4
EPS = 1e-6
NH = 8
WIN_COLS = 2368
WUQ_COLS = 1536
SM_SCALE = float(96 ** -0.5)
DUMN = 128
FILLER = True

V_BADA = 0
V_FFN1N = V_BADA + 288
V_MIXN = V_FFN1N + 32
V_FFN2N = V_MIXN + 32
V_FINALN = V_FFN2N + 32
V_QAN = V_FINALN + 8
V_KVAN = V_QAN + 12
V_CONVW = V_KVAN + 8
V_AON = V_CONVW + 48
V_CON = V_AON + 32
V_CT = V_CON + 16
V_INVF = V_CT + 16
V_SGN = V_INVF + 1
V_HALFPI = V_SGN + 1
V_EPS = V_HALFPI + 1
NV = V_EPS + 1

MAGIC = 12582912.0
TWO_PI = 2.0 * np.pi
C1 = 6.28125
C2 = float(TWO_PI - 6.28125)
PI_LO = 3.1415925


class Op:
    __slots__ = ("eng", "fn", "deps", "sig", "cnt", "is_dma", "dsem", "dval", "qi")


class Sched:
    NSEM = 16

    def __init__(self):
        self.ops = []
        self.lastw = {}
        self.rd = {}
        self.rd_dma = {}

    def add(self, eng, fn, r=(), w=(), dma=False):
        op = Op()
        op.eng = eng
        op.fn = fn
        op.is_dma = dma
        op.sig = False
        op.cnt = 0
        deps = {}
        for k in r:
            o = self.lastw.get(k)
            if o is not None:
                deps[id(o)] = o
        for k in w:
            o = self.lastw.get(k)
            if o is not None:
                deps[id(o)] = o
            for o in self.rd.get(k, {}).values():
                deps[id(o)] = o
            for o in self.rd_dma.get(k, ()):
                deps[id(o)] = o
        op.deps = list(deps.values())
        for k in r:
            if dma:
                self.rd_dma.setdefault(k, []).append(op)
            else:
                self.rd.setdefault(k, {})[eng] = op
        for k in w:
            self.lastw[k] = op
            self.rd[k] = {}
            self.rd_dma[k] = []
        self.ops.append(op)
        return op

    def finalize(self):
        qcount = {}
        for op in self.ops:
            for d in op.deps:
                if d.is_dma:
                    continue
                if d.eng != op.eng or op.is_dma or d.eng != "pe":
                    d.sig = True
            if op.is_dma:
                qi = qcount.get(op.eng, 0)
                qcount[op.eng] = qi + 1
                op.qi = qi
        cnt = {}
        for op in self.ops:
            if op.sig and not op.is_dma:
                cnt[op.eng] = cnt.get(op.eng, 0) + 1
                op.cnt = cnt[op.eng]

    def emit_engine(self, eng, handle, sems, dma_sems):
        waited = {}
        pool = dma_sems.get(eng)
        for op in self.ops:
            if op.eng != eng:
                continue
            for d in op.deps:
                if d.is_dma:
                    s = dma_sems[d.eng][d.qi % self.NSEM]
                    v = 16 * (d.qi // self.NSEM + 1)
                elif d.eng == eng and eng == "pe" and not op.is_dma:
                    continue
                else:
                    s = sems[d.eng]
                    v = d.cnt
                key = id(s)
                if waited.get(key, 0) >= v:
                    continue
                handle.wait_ge(s, v)
                waited[key] = v
            if op.is_dma:
                s = pool[op.qi % self.NSEM]
                if op.qi >= self.NSEM:
                    v = 16 * (op.qi // self.NSEM)
                    if waited.get(id(s), 0) < v:
                        handle.wait_ge(s, v)
                        waited[id(s)] = v
                ins = op.fn(handle)
                ins.then_inc(s, 16)
            else:
                if op.fn is None:
                    continue
                ins = op.fn(handle)
                if op.sig:
                    ins.then_inc(sems[eng], 1)


class Arena:
    def __init__(self, ap, nwords):
        self.ap = ap
        self.n = nwords
        self.off = 0

    def alloc(self, dtype, shape):
        n = 1
        for s in shape:
            n *= s
        words = n if dtype in (F32, I32) else (n + 1) // 2
        a = self.ap[:, self.off:self.off + words]
        self.off += words
        assert self.off <= self.n, f"arena overflow {self.off} > {self.n}"
        if dtype == BF16:
            a = a.bitcast(BF16)
        elif dtype == I32:
            a = a.bitcast(I32)
        if len(shape) == 2:
            a = a.rearrange("p (a b) -> p a b", a=shape[0])
        elif len(shape) == 3:
            a = a.rearrange("p (a b c) -> p a b c", a=shape[0], b=shape[1])
        return a


ARENA_WORDS = 53200


class Builder:
    def __init__(self, cfg):
        self.cfg = cfg
        self.nlayers = cfg.get("layers", L)
        self.ntiles = cfg.get("tiles", NT)
        self.debug = cfg.get("debug", False)
        self.stop = cfg.get("stop", None)
        self.S = Sched()
        self.bank_rr = 0

    def declare_dram(self, nc):
        def inp(name, shape, dt=F32):
            return nc.dram_tensor(name, shape, dt, kind="ExternalInput").ap()

        self.xT = inp("xT", [D, TOK])
        self.pos = inp("pos", [1, TOK], I32)
        NL = self.nlayers
        self.w_ada = inp("w_ada", [NL, D, 9 * D])
        self.wgu = [inp("wgu1", [NL, D, 2 * DFF]), inp("wgu2", [NL, D, 2 * DFF])]
        self.wd = [inp("wd1", [NL, DFF, D]), inp("wd2", [NL, DFF, D])]
        self.w_in = inp("w_in_p", [NL, D, WIN_COLS])
        self.w_uq = inp("w_uq_p", [NL, 384, WUQ_COLS])
        self.w_ukv = inp("w_ukv_p", [NL, 256, 1024])
        self.w_o = inp("w_o", [NL, D, D])
        self.vecs = inp("vecs", [128, NV])
        self.consts = inp("consts", [128, 384])
        self.yT = nc.dram_tensor("yT", [D, TOK], F32, kind="ExternalOutput").ap()
        kind = "ExternalOutput" if self.debug else "Internal"

        def scr(name, shape, dt):
            return nc.dram_tensor(name, shape, dt, kind=kind).ap()

        self.XW = scr("XW", [D, TOK], F32)
        self.QD = scr("QD", [NT, 96, NH * T], BF16)
        self.KD = scr("KD", [NT, 96, NH * T], BF16)
        self.VD = scr("VD", [NT, 128, 4 * 768], BF16)
        self.CD = scr("CD", [NT, 128, 4 * T], BF16)
        self.CT = scr("CT", [32, TOK], F32)
        self.ST = scr("ST", [32, TOK], F32)

    def bank(self):
        b = 1 + (self.bank_rr % 7)
        self.bank_rr += 1
        return b

    def barrier(self):
        self.S.add("pool", lambda e: e.memset(self.SCR1[0:1, 0:1], 0.0), r=(), w=["ARENA"])

    def add(self, eng, fn, r=(), w=(), dma=False):
        r = tuple(r) + ("ARENA",)
        return self.S.add(eng, fn, r=r, w=w, dma=dma)

    def dump(self, name, ap, shape, dt, keys):
        if not self.debug:
            return
        d = self.nc.dram_tensor("DBG_" + name, list(shape), dt, kind="ExternalOutput").ap()
        self.add("sp", lambda e: e.dma_start(out=d, in_=ap), r=keys, w=["DBG_" + name], dma=True)

    def par(self, l, q, k, b):
        return self.PAR[:, ((l * 9 + q) * 8 + k) * 2 + b:((l * 9 + q) * 8 + k) * 2 + b + 1]

    def vcol(self, c, p0=0, p1=128):
        return self.VEC[p0:p1, c:c + 1]

    def rms_stats(self, chunks, nfeat, RS, rs_key, SQ, kparts=128):
        ps0 = self.PS[:, 0, :]
        n = len(chunks)
        for i, (ap, keys) in enumerate(chunks):
            sq = SQ[0:kparts, i % 2, :]
            self.add("act", lambda e, sq=sq, ap=ap: e.activation(sq, ap, AF.Square),
                     r=keys, w=[f"SQ{i % 2}"])
            self.add("pe", lambda e, sq=sq, i=i: e.matmul(ps0, lhsT=self.ONESB[0:kparts, :], rhs=sq,
                                                          start=(i == 0), stop=(i == n - 1)),
                     r=[f"SQ{i % 2}"], w=["ps0"])
        self.rstd_from_psum(ps0, "ps0", nfeat, RS, rs_key)

    def rstd_from_psum(self, ps, ps_key, nfeat, RS, rs_key):
        self.add("act", lambda e: e.activation(RS, ps, AF.Ln, bias=self.vcol(V_EPS), scale=1.0 / nfeat),
                 r=[ps_key, "VEC"], w=[rs_key])
        self.add("act", lambda e: e.activation(RS, RS, AF.Exp, scale=-0.5), r=[rs_key], w=[rs_key])

    def prep(self):
        A = self.arena
        base = A.off
        CACT = A.alloc(BF16, (16,))
        WA = [A.alloc(BF16, (8, 3072)) for _ in range(2)]
        MOD = A.alloc(F32, (72, 2))
        TMPM = A.alloc(F32, (8, 2))
        POSI = A.alloc(I32, (2048,))
        ANG = A.alloc(F32, (2048,))
        KF = A.alloc(F32, (2048,))
        RR = A.alloc(F32, (2048,))
        CC = A.alloc(F32, (2048,))
        SS = A.alloc(F32, (2048,))
        add = self.add
        VEC = self.VEC
        add("sp", lambda e: e.dma_start(out=VEC, in_=self.vecs[:, :]), w=["VEC"], dma=True)
        add("pool", lambda e: e.dma_start(out=self.ONESB, in_=self.consts[:, 0:128]), w=["ONESB"], dma=True)
        add("pool", lambda e: e.dma_start(out=self.IDN, in_=self.consts[:, 128:256]), w=["IDN"], dma=True)
        add("pool", lambda e: e.dma_start(out=self.MB, in_=self.consts[:, 256:384]), w=["MB"], dma=True)
        add("sp", lambda e: e.dma_start(out=self.ONESF, in_=self.consts[:, 0:128]), w=["ONESF"], dma=True)
        add("act", lambda e: e.activation(CACT, VEC[:, V_CT:V_CT + 16], AF.Silu), r=["VEC"], w=["CACT"])
        psA = self.PS[:, 1, :]
        blk_i = 0
        for l in range(self.nlayers):
            for blk in range(3):
                wa = WA[blk_i % 2]
                wk = f"WA{blk_i % 2}"
                blk_i += 1
                for k in range(8):
                    add("pool", lambda e, wa=wa, k=k, l=l, blk=blk: e.dma_start(
                        out=wa[:, k, :], in_=self.w_ada[l, k * 128:(k + 1) * 128, blk * 3072:(blk + 1) * 3072]),
                        w=[f"{wk}.{k}"], dma=True)
                for j in range(24):
                    jj = blk * 24 + j
                    for k in range(8):
                        add("pe", lambda e, wa=wa, k=k, j=j, jj=jj: e.matmul(
                            psA[:, 2 * jj:2 * jj + 2], lhsT=wa[:, k, j * 128:(j + 1) * 128],
                            rhs=CACT[:, 2 * k:2 * k + 2], start=(k == 0), stop=(k == 7)),
                            r=[f"{wk}.{k}", "CACT"], w=["psA"])
            bada = VEC[:, V_BADA + 72 * l:V_BADA + 72 * (l + 1)]
            add("dve", lambda e, bada=bada: e.tensor_tensor(
                out=MOD, in0=psA[:, 0:144].rearrange("p (a b) -> p a b", b=2),
                in1=bada.rearrange("p (a b) -> p a b", b=1).to_broadcast([128, 72, 2]), op=ALU.add),
                r=["psA", "VEC"], w=["MOD"])
            norm_offs = [V_FFN1N, V_MIXN, V_FFN2N]
            coefs = [0.5, 1.0, 0.5]
            for i in range(3):
                nrm = VEC[:, norm_offs[i] + 8 * l:norm_offs[i] + 8 * (l + 1)]
                pa = self.PAR[:, ((l * 9 + 3 * i) * 16):((l * 9 + 3 * i) * 16) + 16].rearrange("p (a b) -> p a b", b=2)
                pb = self.PAR[:, ((l * 9 + 3 * i + 1) * 16):((l * 9 + 3 * i + 1) * 16) + 16].rearrange("p (a b) -> p a b", b=2)
                pg = self.PAR[:, ((l * 9 + 3 * i + 2) * 16):((l * 9 + 3 * i + 2) * 16) + 16].rearrange("p (a b) -> p a b", b=2)
                sh = MOD[:, (3 * i) * 8:(3 * i + 1) * 8, :]
                sc = MOD[:, (3 * i + 1) * 8:(3 * i + 2) * 8, :]
                gt = MOD[:, (3 * i + 2) * 8:(3 * i + 3) * 8, :]
                add("dve", lambda e, sc=sc: e.tensor_scalar(out=TMPM, in0=sc, scalar1=1.0, scalar2=None, op0=ALU.add),
                    r=["MOD"], w=["TMPM"])
                add("dve", lambda e, pa=pa, nrm=nrm: e.tensor_tensor(
                    out=pa, in0=TMPM, in1=nrm.rearrange("p (a b) -> p a b", b=1).to_broadcast([128, 8, 2]), op=ALU.mult),
                    r=["TMPM", "VEC"], w=["PAR"])
                add("dve", lambda e, pb=pb, sh=sh: e.tensor_copy(out=pb, in_=sh), r=["MOD"], w=["PAR"])
                add("dve", lambda e, pg=pg, gt=gt, cf=coefs[i]: e.tensor_scalar(
                    out=pg, in0=gt, scalar1=1.0, scalar2=cf, op0=ALU.add, op1=ALU.mult), r=["MOD"], w=["PAR"])
        P0, P1 = 64, 96
        ncols = self.ntiles * T
        for c0 in range(0, ncols, 2048):
            n = min(2048, ncols - c0)
            add("sp", lambda e, c0=c0, n=n: e.dma_start(out=POSI[P0:P1, 0:n],
                                                        in_=self.pos[0:1, c0:c0 + n].partition_broadcast(32)),
                w=["POSI"], dma=True)
            add("dve", lambda e, n=n: e.tensor_copy(out=ANG[P0:P1, 0:n], in_=POSI[P0:P1, 0:n]), r=["POSI"], w=["ANG"])
            add("dve", lambda e, n=n: e.tensor_scalar(out=ANG[P0:P1, 0:n], in0=ANG[P0:P1, 0:n],
                                                      scalar1=self.vcol(V_INVF, P0, P1), scalar2=None, op0=ALU.mult),
                r=["ANG", "VEC"], w=["ANG"])
            add("dve", lambda e, n=n: e.tensor_scalar(out=KF[P0:P1, 0:n], in0=ANG[P0:P1, 0:n],
                                                      scalar1=float(1.0 / TWO_PI), scalar2=MAGIC, op0=ALU.mult, op1=ALU.add),
                r=["ANG"], w=["KF"])
            add("dve", lambda e, n=n: e.tensor_scalar(out=KF[P0:P1, 0:n], in0=KF[P0:P1, 0:n],
                                                      scalar1=-MAGIC, scalar2=None, op0=ALU.add),
                r=["KF"], w=["KF"])
            add("dve", lambda e, n=n: e.scalar_tensor_tensor(out=RR[P0:P1, 0:n], in0=KF[P0:P1, 0:n], scalar=-C1,
                                                             in1=ANG[P0:P1, 0:n], op0=ALU.mult, op1=ALU.add),
                r=["KF", "ANG"], w=["RR"])
            add("dve", lambda e, n=n: e.scalar_tensor_tensor(out=RR[P0:P1, 0:n], in0=KF[P0:P1, 0:n], scalar=-C2,
                                                             in1=RR[P0:P1, 0:n], op0=ALU.mult, op1=ALU.add),
                r=["KF", "RR"], w=["RR"])
            add("dve", lambda e, n=n: e.tensor_scalar(out=RR[P0:P1, 0:n], in0=RR[P0:P1, 0:n],
                                                      scalar1=-PI_LO, scalar2=PI_LO, op0=ALU.max, op1=ALU.min),
                r=["RR"], w=["RR"])
            add("act", lambda e, n=n: e.activation(SS[P0:P1, 0:n], RR[P0:P1, 0:n], AF.Sin,
                                                   scale=self.vcol(V_SGN, P0, P1)),
                r=["RR", "VEC"], w=["SS"])
            add("act", lambda e, n=n: e.activation(CC[P0:P1, 0:n], RR[P0:P1, 0:n], AF.Sin, scale=0.5),
                r=["RR"], w=["CC"])
            add("dve", lambda e, n=n: e.tensor_tensor(out=CC[P0:P1, 0:n], in0=CC[P0:P1, 0:n], in1=CC[P0:P1, 0:n],
                                                      op=ALU.mult), r=["CC"], w=["CC"])
            add("dve", lambda e, n=n: e.tensor_scalar(out=CC[P0:P1, 0:n], in0=CC[P0:P1, 0:n],
                                                      scalar1=-2.0, scalar2=1.0, op0=ALU.mult, op1=ALU.add),
                r=["CC"], w=["CC"])
            add("sp", lambda e, c0=c0, n=n: e.dma_start(out=self.CT[:, c0:c0 + n], in_=CC[P0:P1, 0:n]),
                r=["CC"], w=["CTd"], dma=True)
            add("sp", lambda e, c0=c0, n=n: e.dma_start(out=self.ST[:, c0:c0 + n], in_=SS[P0:P1, 0:n]),
                r=["SS"], w=["STd"], dma=True)
        self.dump("PAR", self.PAR, (128, L * 9 * 16), F32, ["PAR"])
        self.dump("CACT", CACT, (128, 16), BF16, ["CACT"])
        A.off = base

    def norm_to_h(self, X, xk, H, l, sub, b, RS, SQ, TMP):
        add = self.add
        self.rms_stats([(X[:, k, :], [f"{xk}.{k}"]) for k in range(8)], D, RS, "RS", SQ)
        for k in range(8):
            tmp = TMP[:, k % 2, :]
            add("dve", lambda e, tmp=tmp, k=k: e.scalar_tensor_tensor(
                out=tmp, in0=X[:, k, :], scalar=self.par(l, 3 * sub, k, b), in1=RS, op0=ALU.mult, op1=ALU.mult),
                r=[f"{xk}.{k}", "RS", "PAR"], w=[f"TMP{k % 2}"])
            add("act", lambda e, tmp=tmp, k=k: e.activation(
                H[:, k, :], tmp, AF.Identity, bias=self.par(l, 3 * sub + 1, k, b)),
                r=[f"TMP{k % 2}", "PAR"], w=[f"H.{k}"])

    def f_pass(self, l, which, src, dst, final=False):
        A = self.arena
        base = A.off
        add = self.add
        WGU = A.alloc(BF16, (8, 2 * DFF))
        WD = A.alloc(BF16, (NF, D))
        XB = [A.alloc(F32, (8, T)) for _ in range(2)]
        H = A.alloc(BF16, (8, T))
        ACTB = A.alloc(BF16, (NF, T))
        SQ = A.alloc(BF16, (2, T))
        RS = A.alloc(F32, (T,))
        TMP = A.alloc(F32, (2, T))
        sub = 0 if which == 0 else 2
        wgu = self.wgu[which]
        wd = self.wd[which]
        wguv = wgu[l].rearrange("(k p) n -> p k n", p=128)
        for f0 in range(0, NF, 6):
            f1 = min(NF, f0 + 6)
            for half in range(2):
                c0 = half * DFF + f0 * 128
                c1 = half * DFF + f1 * 128
                add("pool", lambda e, c0=c0, c1=c1: e.dma_start(out=WGU[:, :, c0:c1], in_=wguv[:, :, c0:c1]),
                    w=[f"WGU{half}.{f}" for f in range(f0, f1)], dma=True)
        for f0 in range(0, NF, 2):
            add("pool", lambda e, f0=f0: e.dma_start(
                out=WD[:, f0:f0 + 2, :],
                in_=wd[l, f0 * 128:(f0 + 2) * 128, :].rearrange("(f p) d -> p f d", p=128)),
                w=[f"WD.{f0}", f"WD.{f0 + 1}"], dma=True)
        srcv = src.rearrange("(k p) t -> p k t", p=128)
        dstv = dst.rearrange("(k p) t -> p k t", p=128)
        PS = self.PS

        def load(t):
            xb = t % 2
            add("sp", lambda e: e.dma_start(out=XB[xb], in_=srcv[:, :, t * T:(t + 1) * T]),
                r=[f"XD.{t}"], w=[f"X{xb}.{k}" for k in range(8)], dma=True)

        load(0)

        def do_tile(t):
            xb = t % 2
            X = XB[xb]
            xk = f"X{xb}"
            b = t // NTS
            if t + 1 < self.ntiles:
                load(t + 1)
            if t == 0:
                self.norm_to_h(X, xk, H, l, sub, b, RS, SQ, TMP)
            for f in range(NF):
                g = f % 2
                for k in range(8):
                    add("pe", lambda e, f=f, k=k, g=g: e.matmul(
                        PS[:, 1 + g, :], lhsT=WGU[:, k, f * 128:(f + 1) * 128], rhs=H[:, k, :],
                        start=(k == 0), stop=(k == 7)), r=[f"WGU0.{f}", f"H.{k}"], w=[f"psG{g}"])
                for k in range(8):
                    add("pe", lambda e, f=f, k=k, g=g: e.matmul(
                        PS[:, 3 + g, :], lhsT=WGU[:, k, DFF + f * 128:DFF + (f + 1) * 128], rhs=H[:, k, :],
                        start=(k == 0), stop=(k == 7)), r=[f"WGU1.{f}", f"H.{k}"], w=[f"psU{g}"])
                add("act", lambda e, g=g: e.activation(TMP[:, g, :], PS[:, 1 + g, :], AF.Silu),
                    r=[f"psG{g}"], w=[f"TMP{g}"])
                add("dve", lambda e, f=f, g=g: e.tensor_tensor(out=ACTB[:, f, :], in0=TMP[:, g, :], in1=PS[:, 3 + g, :],
                                                               op=ALU.mult),
                    r=[f"TMP{g}", f"psU{g}"], w=[f"ACTB.{f}"])
            if t + 1 < self.ntiles:
                self.norm_to_h(XB[(t + 1) % 2], f"X{(t + 1) % 2}", H, l, sub, (t + 1) // NTS, RS, SQ, TMP)
            for d in range(8):
                yb = d % 2
                for f in range(NF):
                    add("pe", lambda e, f=f, d=d, yb=yb: e.matmul(
                        PS[:, 5 + yb, :], lhsT=WD[:, f, d * 128:(d + 1) * 128], rhs=ACTB[:, f, :],
                        start=(f == 0), stop=(f == NF - 1)), r=[f"WD.{f}", f"ACTB.{f}"], w=[f"psY{yb}"])
                add("dve", lambda e, d=d, yb=yb: e.scalar_tensor_tensor(
                    out=X[:, d, :], in0=PS[:, 5 + yb, :], scalar=self.par(l, 3 * sub + 2, d, b), in1=X[:, d, :],
                    op0=ALU.mult, op1=ALU.add), r=[f"psY{yb}", f"{xk}.{d}", "PAR"], w=[f"{xk}.{d}"])
            if final:
                self.rms_stats([(X[:, k, :], [f"{xk}.{k}"]) for k in range(8)], D, RS, "RS", SQ)
                for k in range(8):
                    add("dve", lambda e, k=k: e.scalar_tensor_tensor(
                        out=X[:, k, :], in0=X[:, k, :], scalar=self.vcol(V_FINALN + k), in1=RS,
                        op0=ALU.mult, op1=ALU.mult), r=[f"{xk}.{k}", "RS", "VEC"], w=[f"{xk}.{k}"])
            o = add("sp", lambda e, t=t, X=X: e.dma_start(out=dstv[:, :, t * T:(t + 1) * T], in_=X),
                    r=[f"{xk}.{k}" for k in range(8)], w=[f"XD.{t}"], dma=True)
            if final:
                self.out_dmas.append(o)

        for t in range(self.ntiles):
            do_tile(t)
        A.off = base

    def m1_pass(self, l):
        A = self.arena
        base = A.off
        add = self.add
        PS = self.PS
        WIN = A.alloc(BF16, (8, WIN_COLS))
        WUQ = A.alloc(BF16, (3, WUQ_COLS))
        WUKV = A.alloc(BF16, (2, 1024))
        XB = [A.alloc(F32, (8, T)) for _ in range(2)]
        H = A.alloc(BF16, (8, T))
        CQN = A.alloc(BF16, (3, T))
        CKVN = A.alloc(BF16, (2, T))
        CS = A.alloc(F32, (2, T))
        SQ = A.alloc(BF16, (2, T))
        RS = A.alloc(F32, (T,))
        RS2 = A.alloc(F32, (T,))
        TMP = A.alloc(F32, (2, T))
        U = A.alloc(F32, (4, T + 2))
        GB = A.alloc(F32, (4, T))
        CV = A.alloc(F32, (4, T))
        YC = A.alloc(F32, (2, T))
        CG = A.alloc(BF16, (4, T))
        T1B = A.alloc(F32, (2, T))
        T2B = A.alloc(F32, (2, T))
        KR = A.alloc(BF16, (T,))
        QS = A.alloc(BF16, (NH, T))
        KS = A.alloc(BF16, (NH, T))
        VS = A.alloc(BF16, (4, 4, 192))
        for k in range(8):
            add("pool", lambda e, k=k: e.dma_start(out=WIN[:, k, :], in_=self.w_in[l, k * 128:(k + 1) * 128, :]),
                w=[f"WIN.{k}"], dma=True)
        add("pool", lambda e: e.dma_start(out=WUQ, in_=self.w_uq[l].rearrange("(k p) n -> p k n", p=128)),
            w=["WUQ"], dma=True)
        add("pool", lambda e: e.dma_start(out=WUKV, in_=self.w_ukv[l].rearrange("(k p) n -> p k n", p=128)),
            w=["WUKV"], dma=True)
        add("dve", lambda e: e.memset(VS, 1.0), w=["VS"])
        srcv = self.XW.rearrange("(k p) t -> p k t", p=128)

        def load(t):
            xb = t % 2
            add("sp", lambda e: e.dma_start(out=XB[xb], in_=srcv[:, :, t * T:(t + 1) * T]),
                r=[f"XD.{t}"], w=[f"X{xb}.{k}" for k in range(8)], dma=True)

        def proj(col0, ncol, bank, tag):
            for k in range(8):
                add("pe", lambda e, k=k: e.matmul(PS[0:ncol, bank, :], lhsT=WIN[:, k, col0:col0 + ncol], rhs=H[:, k, :],
                                                  start=(k == 0), stop=(k == 7)),
                    r=[f"WIN.{k}", f"H.{k}"], w=[f"ps{bank}"])

        load(0)

        def do_tile(t):
            xb = t % 2
            X = XB[xb]
            xk = f"X{xb}"
            b = t // NTS
            j = t % NTS
            if t + 1 < self.ntiles:
                load(t + 1)
            add("sp", lambda e, t=t: e.dma_start(out=CS[64:96, 0, :], in_=self.CT[:, t * T:(t + 1) * T]),
                r=["CTd"], w=["CS0"], dma=True)
            add("sp", lambda e, t=t: e.dma_start(out=CS[64:96, 1, :], in_=self.ST[:, t * T:(t + 1) * T]),
                r=["STd"], w=["CS1"], dma=True)
            if t == 0:
                self.norm_to_h(X, xk, H, l, 1, b, RS, SQ, TMP)
            for (col0, nch, DST, dk, gcol, nfeat) in ((0, 3, CQN, "CQN", V_QAN + 3 * l, 384),
                                                       (384, 2, CKVN, "CKVN", V_KVAN + 2 * l, 256)):
                banks = []
                for m in range(nch):
                    bk = self.bank()
                    banks.append(bk)
                    proj(col0 + m * 128, 128, bk, dk)
                self.rms_stats([(PS[:, bk, :], [f"ps{bk}"]) for bk in banks], nfeat, RS, "RS", SQ)
                for m, bk in enumerate(banks):
                    add("dve", lambda e, m=m, bk=bk, DST=DST, gcol=gcol: e.scalar_tensor_tensor(
                        out=DST[:, m, :], in0=PS[:, bk, :], scalar=self.vcol(gcol + m), in1=RS,
                        op0=ALU.mult, op1=ALU.mult), r=[f"ps{bk}", "RS", "VEC"], w=[f"{dk}.{m}"])
            bk1 = self.bank()
            proj(2176, 96, bk1, "kpe")
            bk2 = self.bank()
            proj(2272, 96, bk2, "kpesw")
            add("dve", lambda e, bk1=bk1: e.tensor_tensor(out=T1B[64:96, 0, :], in0=PS[64:96, bk1, :], in1=CS[64:96, 0, :],
                                                           op=ALU.mult), r=[f"ps{bk1}", "CS0"], w=["T1.0"])
            add("dve", lambda e, bk2=bk2: e.tensor_tensor(out=T2B[64:96, 0, :], in0=PS[64:96, bk2, :], in1=CS[64:96, 1, :],
                                                           op=ALU.mult), r=[f"ps{bk2}", "CS1"], w=["T2.0"])
            add("dve", lambda e: e.tensor_tensor(out=KR[64:96, :], in0=T1B[64:96, 0, :], in1=T2B[64:96, 0, :], op=ALU.add),
                r=["T1.0", "T2.0"], w=["KR"])
            add("sp", lambda e, t=t: e.dma_start(
                out=self.KD[t].rearrange("p (a b) -> p a b", a=NH)[64:96],
                in_=KR[64:96, None, :].to_broadcast([32, NH, T])),
                r=["KR"], w=[f"KDr.{t}"], dma=True)
            if j == 0:
                add("pool", lambda e: e.memset(U[:, :, 0:2], 0.0), w=[f"U.{c}" for c in range(4)])
            for c in range(4):
                bgc = self.bank()
                proj(640 + 512 + c * 128, 128, bgc, "gc")
                bvl = self.bank()
                proj(640 + 1024 + c * 128, 128, bvl, "val")
                bgb = self.bank()
                proj(640 + c * 128, 128, bgb, "gb")
                add("act", lambda e, bgc=bgc: e.activation(TMP[:, 0, :], PS[:, bgc, :], AF.Copy),
                    r=[f"ps{bgc}"], w=["TMP0"])
                add("dve", lambda e, c=c, bvl=bvl: e.tensor_tensor(out=U[:, c, 2:T + 2], in0=TMP[:, 0, :],
                                                                   in1=PS[:, bvl, :], op=ALU.mult),
                    r=["TMP0", f"ps{bvl}"], w=[f"U.{c}"])
                add("act", lambda e, c=c, bgb=bgb: e.activation(GB[:, c, :], PS[:, bgb, :], AF.Copy),
                    r=[f"ps{bgb}"], w=[f"GB.{c}"])
                yc = YC[:, c % 2, :]
                wc = V_CONVW + 12 * l
                add("pool", lambda e, c=c, yc=yc, wc=wc: e.tensor_scalar(
                    out=yc, in0=U[:, c, 2:T + 2], scalar1=self.vcol(wc + 8 + c), scalar2=0.0, op0=ALU.mult, op1=ALU.add),
                    r=[f"U.{c}", "VEC"], w=[f"YC{c % 2}"])
                add("dve", lambda e, c=c, yc=yc, wc=wc: e.scalar_tensor_tensor(
                    out=yc, in0=U[:, c, 1:T + 1], scalar=self.vcol(wc + 4 + c), in1=yc, op0=ALU.mult, op1=ALU.add),
                    r=[f"U.{c}", "VEC", f"YC{c % 2}"], w=[f"YC{c % 2}"])
                add("dve", lambda e, c=c, yc=yc, wc=wc: e.scalar_tensor_tensor(
                    out=yc, in0=U[:, c, 0:T], scalar=self.vcol(wc + c), in1=yc, op0=ALU.mult, op1=ALU.add),
                    r=[f"U.{c}", "VEC", f"YC{c % 2}"], w=[f"YC{c % 2}"])
                add("pool", lambda e, c=c, yc=yc: e.tensor_tensor(out=CV[:, c, :], in0=yc, in1=GB[:, c, :], op=ALU.mult),
                    r=[f"YC{c % 2}", f"GB.{c}"], w=[f"CV.{c}"])
                add("pool", lambda e, c=c: e.tensor_copy(out=U[:, c, 0:2], in_=U[:, c, T:T + 2]),
                    r=[f"U.{c}"], w=[f"U.{c}"])
            if t + 1 < self.ntiles:
                self.norm_to_h(XB[(t + 1) % 2], f"X{(t + 1) % 2}", H, l, 1, (t + 1) // NTS, RS, SQ, TMP)
            for h in range(NH):
                ba = self.bank()
                for k in range(3):
                    add("pe", lambda e, k=k, h=h, ba=ba: e.matmul(
                        PS[0:96, ba, :], lhsT=WUQ[:, k, h * 192:h * 192 + 96], rhs=CQN[:, k, :],
                        start=(k == 0), stop=(k == 2)), r=["WUQ", f"CQN.{k}"], w=[f"ps{ba}"])
                bb = self.bank()
                for k in range(3):
                    add("pe", lambda e, k=k, h=h, bb=bb: e.matmul(
                        PS[0:96, bb, :], lhsT=WUQ[:, k, h * 192 + 96:h * 192 + 192], rhs=CQN[:, k, :],
                        start=(k == 0), stop=(k == 2)), r=["WUQ", f"CQN.{k}"], w=[f"ps{bb}"])
                add("act", lambda e, h=h, ba=ba: e.activation(QS[0:64, h, :], PS[0:64, ba, :], AF.Copy),
                    r=[f"ps{ba}"], w=["QS"])
                tb = h % 2
                add("dve", lambda e, ba=ba, tb=tb: e.tensor_tensor(out=T1B[64:96, tb, :], in0=PS[64:96, ba, :],
                                                                    in1=CS[64:96, 0, :], op=ALU.mult),
                    r=[f"ps{ba}", "CS0"], w=[f"T1.{tb}"])
                add("dve", lambda e, bb=bb, tb=tb: e.tensor_tensor(out=T2B[64:96, tb, :], in0=PS[64:96, bb, :],
                                                                    in1=CS[64:96, 1, :], op=ALU.mult),
                    r=[f"ps{bb}", "CS1"], w=[f"T2.{tb}"])
                add("pool", lambda e, h=h, tb=tb: e.tensor_tensor(out=QS[64:96, h, :], in0=T1B[64:96, tb, :],
                                                                   in1=T2B[64:96, tb, :], op=ALU.add),
                    r=[f"T1.{tb}", f"T2.{tb}"], w=["QS"])
            for h in range(NH):
                bk = self.bank()
                for k in range(2):
                    add("pe", lambda e, k=k, h=h, bk=bk: e.matmul(
                        PS[0:64, bk, :], lhsT=WUKV[:, k, h * 64:(h + 1) * 64], rhs=CKVN[:, k, :],
                        start=(k == 0), stop=(k == 1)), r=["WUKV", f"CKVN.{k}"], w=[f"ps{bk}"])
                add("act", lambda e, h=h, bk=bk: e.activation(KS[0:64, h, :], PS[0:64, bk, :], AF.Copy),
                    r=[f"ps{bk}"], w=["KSn"])
            for c4 in range(4):
                bk = self.bank()
                for k in range(2):
                    add("pe", lambda e, k=k, c4=c4, bk=bk: e.matmul(
                        PS[:, bk, :], lhsT=CKVN[:, k, c4 * 128:(c4 + 1) * 128], rhs=WUKV[:, k, 512:1024],
                        start=(k == 0), stop=(k == 1)), r=["WUKV", f"CKVN.{k}"], w=[f"ps{bk}"])
                pv4 = PS[:, bk, :].rearrange("p (a s d) -> p a s d", a=4, s=2)
                add("dve", lambda e, c4=c4, pv4=pv4: e.tensor_copy(out=VS[:, c4, :, 0:64], in_=pv4[:, :, 0, :]),
                    r=[f"ps{bk}"], w=["VS"])
                add("act", lambda e, c4=c4, pv4=pv4: e.activation(VS[:, c4, :, 128:192], pv4[:, :, 1, :], AF.Copy),
                    r=[f"ps{bk}"], w=["VS"])
            self.rms_stats([(CV[:, c, :], [f"CV.{c}"]) for c in range(4)], 512, RS2, "RS2", SQ)
            for c in range(4):
                add("dve", lambda e, c=c: e.scalar_tensor_tensor(
                    out=CG[:, c, :], in0=CV[:, c, :], scalar=self.vcol(V_CON + 4 * l + c), in1=RS2,
                    op0=ALU.mult, op1=ALU.mult), r=[f"CV.{c}", "VEC", "RS2"], w=["CG"])
            add("sp", lambda e, t=t: e.dma_start(out=self.CD[t].rearrange("p (a b) -> p a b", a=4), in_=CG),
                r=["CG"], w=[f"CDd.{t}"], dma=True)
            add("sp", lambda e, t=t: e.dma_start(out=self.QD[t].rearrange("p (a b) -> p a b", a=NH), in_=QS[0:96]),
                r=["QS"], w=[f"QDd.{t}"], dma=True)
            add("sp", lambda e, t=t: e.dma_start(out=self.KD[t].rearrange("p (a b) -> p a b", a=NH)[0:64], in_=KS[0:64]),
                r=["KSn"], w=[f"KDd.{t}"], dma=True)
            add("sp", lambda e, t=t: e.dma_start(out=self.VD[t].rearrange("p (a b c) -> p a b c", a=4, b=4), in_=VS),
                r=["VS"], w=[f"VDd.{t}"], dma=True)

        for t in range(self.ntiles):
            do_tile(t)
        A.off = base

    def m2_pass(self, l):
        A = self.arena
        base = A.off
        add = self.add
        PS = self.PS
        KT = A.alloc(BF16, (NH, SEQ))
        VT = A.alloc(BF16, (32, 768))
        WO = A.alloc(BF16, (8, D))
        QTB = [A.alloc(BF16, (NH, T)) for _ in range(2)]
        CGB = [A.alloc(BF16, (4, T)) for _ in range(2)]
        X = A.alloc(F32, (8, T))
        ANF = A.alloc(F32, (4, T))
        AG = A.alloc(BF16, (4, T))
        PT = A.alloc(BF16, (4, T))
        RBS = A.alloc(F32, (T,))
        SQ = A.alloc(BF16, (4, T))
        RSA = A.alloc(F32, (T,))
        add("pool", lambda e: e.dma_start(out=WO, in_=self.w_o[l].rearrange("(k p) n -> p k n", p=128)),
            w=["WO"], dma=True)
        xv = self.XW.rearrange("(k p) t -> p k t", p=128)
        ST_B = (0, 1, 4)
        OT_B = (2, 3)
        SSA_B = 5
        Y_B = (6, 6)
        DUM_B = 7
        NQA = 2

        def load(t):
            qb = t % 2
            add("sp", lambda e: e.dma_start(out=QTB[qb][0:96], in_=self.QD[t].rearrange("p (a b) -> p a b", a=NH)),
                r=[f"QDd.{t}"], w=[f"QT{qb}"], dma=True)
            add("sp", lambda e: e.dma_start(out=CGB[qb], in_=self.CD[t].rearrange("p (a b) -> p a b", a=4)),
                r=[f"CDd.{t}"], w=[f"CGB{qb}"], dma=True)

        def load_kv(t):
            j = t % NTS
            add("sp", lambda e: e.dma_start(out=KT[0:96, :, j * T:(j + 1) * T],
                                            in_=self.KD[t].rearrange("p (a b) -> p a b", a=NH)),
                r=[f"KDd.{t}", f"KDr.{t}"], w=[f"KT.{j}"], dma=True)
            add("sp", lambda e: e.dma_start(out=VT[:, 4 * j:4 * j + 4, :],
                                            in_=self.VD[t].rearrange("p (a b) -> p a b", a=4)),
                r=[f"VDd.{t}"], w=[f"VT.{j}"], dma=True)

        carry = []

        def tile_end(t, qb, b):
            def part0():
                for pr in range(4):
                    add("pe", lambda e, pr=pr: e.matmul(PS[:, SSA_B, :], lhsT=self.ONESB, rhs=SQ[:, pr, :],
                                                        start=(pr == 0), stop=(pr == 3)),
                        r=[f"SQ{pr}", "ONESB"], w=["psSSA"])
                self.rstd_from_psum(PS[:, SSA_B, :], "psSSA", 512, RSA, "RSA")
                for pr in range(4):
                    add("dve", lambda e, pr=pr: e.scalar_tensor_tensor(
                        out=AG[:, pr, :], in0=ANF[:, pr, :], scalar=self.vcol(V_AON + 8 * l + pr), in1=RSA,
                        op0=ALU.mult, op1=ALU.mult), r=[f"ANF.{pr}", "RSA", "VEC"], w=[f"AG.{pr}"])
            carry.append(part0)

            def wo_mm(d, k):
                yb = Y_B[d % 2]
                rhs = AG[:, k, :] if k < 4 else CGB[qb][:, k - 4, :]
                rk = f"AG.{k}" if k < 4 else f"CGB{qb}"
                add("pe", lambda e: e.matmul(PS[:, yb, :], lhsT=WO[:, k, d * 128:(d + 1) * 128], rhs=rhs,
                                             start=(k == 0), stop=(k == 7)), r=["WO", rk], w=[f"psY{yb}"])
                if k == 7:
                    add("dve", lambda e: e.scalar_tensor_tensor(
                        out=X[:, d, :], in0=PS[:, yb, :], scalar=self.par(l, 5, d, b), in1=X[:, d, :],
                        op0=ALU.mult, op1=ALU.add), r=[f"psY{yb}", f"X.{d}", "PAR"], w=[f"X.{d}"])
            for d in range(8):
                for k in range(8):
                    carry.append(lambda d=d, k=k: wo_mm(d, k))

            def fin():
                add("sp", lambda e: e.dma_start(out=xv[:, :, t * T:(t + 1) * T], in_=X),
                    r=[f"X.{k}" for k in range(8)], w=[f"XD.{t}"], dma=True)
            carry.append(fin)

        def load_x(t):
            add("sp", lambda e: e.dma_start(out=X, in_=xv[:, :, t * T:(t + 1) * T]),
                r=[f"XD.{t}"], w=[f"X.{k}" for k in range(8)], dma=True)

        load(0)
        load_x(0)

        def do_tile(t):
            qb = t % 2
            QT = QTB[qb]
            b = t // NTS
            j = t % NTS
            load_kv(t)
            nch = 4 * j + 4
            blocks = [(h, c) for h in range(NH) for c in range(nch)]
            nb = len(blocks)
            quota = -(-66 // (nb - 8))
            if t == 0 and t + 1 < self.ntiles:
                load(t + 1)

            def qk(i):
                h, c = blocks[i]
                lo = max(0, c - 4 * j) * 128
                sb = ST_B[i % 3]
                diag = c >= 4 * j
                add("pe", lambda e: e.matmul(PS[:, sb, lo:T], lhsT=KT[0:96, h, c * 128:(c + 1) * 128],
                                             rhs=QT[0:96, h, lo:T], start=True, stop=not diag),
                    r=[f"KT.{c // 4}", f"QT{qb}"], w=[f"psST{i % 3}"])
                if diag:
                    add("pe", lambda e: e.matmul(PS[:, sb, lo:lo + 128], lhsT=self.IDN, rhs=self.MB,
                                                 start=False, stop=True),
                        r=["IDN", "MB"], w=[f"psST{i % 3}"])
                add("act", lambda e: e.activation(PT[:, i % 4, lo:T], PS[:, sb, lo:T], AF.Exp, scale=SM_SCALE),
                    r=[f"psST{i % 3}"], w=[f"PT{i % 4}"])

            def head_end(h):
                ob = OT_B[h % 2]
                pr = h // 2
                if h % 2 == 0:
                    o_lo, o_hi, s_lo, s_hi = 0, 64, 64, 128
                else:
                    o_lo, o_hi, s_lo, s_hi = 64, 128, 0, 64
                add("dve", lambda e: e.reciprocal(RBS[o_lo:o_hi, :], PS[s_lo:s_hi, ob, :]),
                    r=[f"psOT{h % 2}"], w=["RBS"])
                add("dve", lambda e: e.tensor_tensor(out=ANF[o_lo:o_hi, pr, :], in0=PS[o_lo:o_hi, ob, :],
                                                     in1=RBS[o_lo:o_hi, :], op=ALU.mult),
                    r=[f"psOT{h % 2}", "RBS"], w=[f"ANF.{pr}"])
                add("pool", lambda e: e.tensor_tensor(out=SQ[o_lo:o_hi, pr, :], in0=ANF[o_lo:o_hi, pr, :],
                                                      in1=ANF[o_lo:o_hi, pr, :], op=ALU.mult),
                    r=[f"ANF.{pr}"], w=[f"SQ{pr}"])

            pend = []

            def pv(i):
                h, c = blocks[i]
                lo = max(0, c - 4 * j) * 128
                ob = OT_B[h % 2]
                pr = h // 2
                c0 = 0 if h % 2 == 0 else 64
                add("pe", lambda e: e.matmul(PS[:, ob, lo:T], lhsT=VT[:, c, pr * 192 + c0:pr * 192 + c0 + 128],
                                             rhs=PT[:, i % 4, lo:T], start=(c == 0), stop=(c == nch - 1)),
                    r=[f"PT{i % 4}", f"VT.{c // 4}"], w=[f"psOT{h % 2}"])
                if c == nch - 1:
                    pend.append([3, h])

            def tick():
                for p in list(pend):
                    p[0] -= 1
                    if p[0] <= 0:
                        pend.remove(p)
                        head_end(p[1])

            for i in range(min(NQA, nb)):
                qk(i)
            for i in range(nb):
                if i + NQA < nb:
                    qk(i + NQA)
                pv(i)
                tick()
                did_filler = False
                if i >= 3 and carry:
                    nq = 1 if i == 3 else quota
                    for _ in range(nq):
                        if carry:
                            carry.pop(0)()
                            did_filler = True
                    if not carry:
                        load_x(t)
                        if t + 1 < self.ntiles:
                            load(t + 1)
                if FILLER and not did_filler:
                    add("pe", lambda e: e.matmul(PS[:, DUM_B, 0:DUMN], lhsT=self.IDN, rhs=PT[:, 0, 0:DUMN],
                                                 start=True, stop=True), r=["IDN"], w=["psDUM"])
            while pend:
                tick()
            tile_end(t, qb, b)

        for t in range(self.ntiles):
            do_tile(t)
        while carry:
            carry.pop(0)()
        A.off = base

    def build(self):
        nc = bass.Bass("TRN2", target_bir_lowering=False)
        self.nc = nc
        self.declare_dram(nc)
        self.out_dmas = []
        with nc.sbuf_tensor("arena", [128, ARENA_WORDS], F32) as arena_t, \
                nc.psum_tensor("ps", [128, 8, 512], F32) as ps_t:
            self.arena = Arena(arena_t, ARENA_WORDS)
            self.PS = ps_t
            A = self.arena
            self.VEC = A.alloc(F32, (NV,))
            self.PAR = A.alloc(F32, (L * 9 * 16,))
            self.ONESB = A.alloc(BF16, (128,))
            self.IDN = A.alloc(BF16, (128,))
            self.MB = A.alloc(BF16, (128,))
            self.ONESF = A.alloc(F32, (128,))
            self.SCR1 = A.alloc(F32, (2,))
            self.prep()
            self.barrier()
            stop = self.stop
            done = False
            for l in range(self.nlayers):
                src = self.xT if l == 0 else self.XW
                for pname in ("F1", "M1", "M2", "F2"):
                    if pname == "F1":
                        self.f_pass(l, 0, src, self.XW)
                    elif pname == "M1":
                        self.m1_pass(l)
                    elif pname == "M2":
                        self.m2_pass(l)
                    else:
                        last = (l == self.nlayers - 1)
                        self.f_pass(l, 1, self.XW, self.yT if last else self.XW, final=last)
                    self.barrier()
                    if stop is not None and stop == (l, pname):
                        done = True
                        break
                if done:
                    break
            self.S.add("sp", None, r=(), w=["ARENA"])
            self.S.finalize()
            with nc.semaphore("s_pe") as s_pe, nc.semaphore("s_act") as s_act, nc.semaphore("s_dve") as s_dve, \
                    nc.semaphore("s_pool") as s_pool, nc.semaphore("s_sp") as s_sp:
                sems = {"pe": s_pe, "act": s_act, "dve": s_dve, "pool": s_pool, "sp": s_sp}
                import contextlib
                with contextlib.ExitStack() as es:
                    dsems = {"sp": [es.enter_context(nc.semaphore(f"d_sp{i}")) for i in range(Sched.NSEM)],
                             "pool": [es.enter_context(nc.semaphore(f"d_pl{i}")) for i in range(Sched.NSEM)]}
                    with nc.Block() as block:
                        @block.tensor
                        def _(e):
                            self.S.emit_engine("pe", e, sems, dsems)

                        @block.scalar
                        def _(e):
                            self.S.emit_engine("act", e, sems, dsems)

                        @block.vector
                        def _(e):
                            self.S.emit_engine("dve", e, sems, dsems)

                        @block.gpsimd
                        def _(e):
                            self.S.emit_engine("pool", e, sems, dsems)

                        @block.sync
                        def _(e):
                            self.S.emit_engine("sp", e, sems, dsems)
        return nc


def _chunks(v, p=128):
    v = np.asarray(v, np.float32)
    return np.ascontiguousarray(v.reshape(-1, p).T)


def make_shared(inputs):
    f32 = np.float32
    w_in = np.asarray(inputs["w_in"], f32)
    cq, ckv, kpe = w_in[:, :, 0:384], w_in[:, :, 384:640], w_in[:, :, 640:672]
    gb, gc, val = w_in[:, :, 672:1184], w_in[:, :, 1184:1696], w_in[:, :, 1696:2208]
    z64 = np.zeros((L, D, 64), f32)
    kpe_sw = np.concatenate([kpe[:, :, 16:32], kpe[:, :, 0:16]], axis=-1)
    w_in_p = np.ascontiguousarray(np.concatenate([cq, ckv, gb, gc, val, z64, kpe, z64, kpe_sw], axis=-1))
    assert w_in_p.shape[-1] == WIN_COLS
    w_uq = np.asarray(inputs["w_uq"], f32).reshape(L, 384, NH, 96)
    nope, pe = w_uq[..., 0:64], w_uq[..., 64:96]
    pe_sw = np.concatenate([pe[..., 16:32], pe[..., 0:16]], axis=-1)
    zq = np.zeros((L, 384, NH, 64), f32)
    w_uq_p = np.ascontiguousarray(np.concatenate([nope, pe, zq, pe_sw], axis=-1).reshape(L, 384, WUQ_COLS))
    w_ukv = np.asarray(inputs["w_ukv"], f32).reshape(L, 256, NH, 128)
    w_ukv_p = np.ascontiguousarray(np.concatenate([w_ukv[..., 0:64].reshape(L, 256, 512),
                                                   w_ukv[..., 64:128].reshape(L, 256, 512)], axis=-1))
    consts = np.zeros((128, 384), f32)
    consts[:, 0:128] = 1.0
    consts[:, 128:256] = np.eye(128, dtype=f32)
    consts[:, 256:384] = np.where(np.arange(128)[:, None] > np.arange(128)[None, :], -30000.0, 0.0)
    shared = {
        "w_ada": np.ascontiguousarray(inputs["w_ada"], f32),
        "wgu1": np.ascontiguousarray(inputs["ffn1_w_gu"], f32), "wd1": np.ascontiguousarray(inputs["ffn1_w_down"], f32),
        "wgu2": np.ascontiguousarray(inputs["ffn2_w_gu"], f32), "wd2": np.ascontiguousarray(inputs["ffn2_w_down"], f32),
        "w_in_p": w_in_p, "w_uq_p": w_uq_p, "w_ukv_p": w_ukv_p,
        "w_o": np.ascontiguousarray(inputs["w_o"], f32), "consts": consts,
    }
    vec = np.zeros((128, NV), f32)
    for l in range(L):
        vec[:, V_BADA + 72 * l:V_BADA + 72 * (l + 1)] = _chunks(inputs["b_ada"][l])
        vec[:, V_FFN1N + 8 * l:V_FFN1N + 8 * (l + 1)] = _chunks(inputs["ffn1_norm"][l])
        vec[:, V_MIXN + 8 * l:V_MIXN + 8 * (l + 1)] = _chunks(inputs["mix_norm"][l])
        vec[:, V_FFN2N + 8 * l:V_FFN2N + 8 * (l + 1)] = _chunks(inputs["ffn2_norm"][l])
        vec[:, V_QAN + 3 * l:V_QAN + 3 * (l + 1)] = _chunks(inputs["q_a_norm"][l])
        vec[:, V_KVAN + 2 * l:V_KVAN + 2 * (l + 1)] = _chunks(inputs["kv_a_norm"][l])
        cw = np.asarray(inputs["conv_w"][l], f32)
        for jt in range(3):
            vec[:, V_CONVW + 12 * l + 4 * jt:V_CONVW + 12 * l + 4 * (jt + 1)] = _chunks(cw[jt])
        vec[:, V_AON + 8 * l:V_AON + 8 * l + 4] = _chunks(inputs["attn_out_norm"][l])
        vec[:, V_CON + 4 * l:V_CON + 4 * (l + 1)] = _chunks(inputs["conv_out_norm"][l])
    vec[:, V_FINALN:V_FINALN + 8] = _chunks(inputs["final_norm"])
    inv_freq = (1.0 / (np.float32(10000.0) ** (np.arange(0, 32, 2, dtype=f32) / np.float32(32)))).astype(f32)
    p = np.arange(128)
    vec[:, V_INVF] = inv_freq[p % 16]
    vec[:, V_SGN] = np.where((p % 32) < 16, -1.0, 1.0)
    vec[:, V_HALFPI] = np.float32(np.pi / 2)
    vec[:, V_EPS] = np.float32(EPS)
    return shared, vec


def make_core_inputs(inputs, shared, vec, i):
    f32 = np.float32
    x = np.asarray(inputs["x"][2 * i:2 * i + 2], f32)
    xT = np.ascontiguousarray(x.transpose(2, 0, 1).reshape(D, TOK))
    pos = np.ascontiguousarray(np.asarray(inputs["positions"][2 * i:2 * i + 2], np.int32).reshape(1, TOK))
    v = vec.copy()
    c = np.asarray(inputs["c"][2 * i:2 * i + 2], f32)
    v[:, V_CT:V_CT + 16] = c.reshape(2, 8, 128).transpose(2, 1, 0).reshape(128, 16)
    m = dict(shared)
    m["xT"] = xT
    m["pos"] = pos
    m["vecs"] = v
    return m


_NC_CACHE = {}


def kernel(**inputs):
    inputs = {k: np.asarray(v) for k, v in inputs.items()}
    shared, vec = make_shared(inputs)
    in_maps = [make_core_inputs(inputs, shared, vec, i) for i in range(NCORES)]
    if "nc" not in _NC_CACHE:
        _NC_CACHE["nc"] = Builder({}).build()
    nc = _NC_CACHE["nc"]
    res = run_bass_kernel_spmd(nc, in_maps, core_ids=list(range(NCORES)))
    out = np.empty((2 * NCORES, SEQ, D), np.float32)
    for i in range(NCORES):
        yT = np.asarray(res.results[i]["yT"], np.float32)
        out[2 * i:2 * i + 2] = yT.reshape(D, 2, SEQ).transpose(1, 2, 0)
    return out
```

```python
import numpy as np
import concourse.bass as bass
import concourse.mybir as mybir
from concourse.bass_utils import run_bass_kernel_spmd

F32 = mybir.dt.float32
BF16 = mybir.dt.bfloat16
I32 = mybir.dt.int32
AF = mybir.ActivationFunctionType
ALU = mybir.AluOpType

NCORES = 8
D = 1024
SEQ = 4096
NSEQ = 2
TOK = NSEQ * SEQ
T = 512
NTS = SEQ // T
NT = NSEQ * NTS
DFF = 2816
NF = 22
L = 4
EPS = 1e-6
NH = 8
WIN_COLS = 2368
WUQ_COLS = 1536
SM_SCALE = float(96 ** -0.5)
DUMN = 128
FILLER = True

V_BADA = 0
V_FFN1N = V_BADA + 288
V_MIXN = V_FFN1N + 32
V_FFN2N = V_MIXN + 32
V_FINALN = V_FFN2N + 32
V_QAN = V_FINALN + 8
V_KVAN = V_QAN + 12
V_CONVW = V_KVAN + 8
V_AON = V_CONVW + 48
V_CON = V_AON + 32
V_CT = V_CON + 16
V_INVF = V_CT + 16
V_SGN = V_INVF + 1
V_HALFPI = V_SGN + 1
V_EPS = V_HALFPI + 1
NV = V_EPS + 1

MAGIC = 12582912.0
TWO_PI = 2.0 * np.pi
C1 = 6.28125
C2 = float(TWO_PI - 6.28125)
PI_LO = 3.1415925


class Op:
    __slots__ = ("eng", "fn", "deps", "sig", "cnt", "is_dma", "dsem", "dval", "qi")


class Sched:
    NSEM = 16

    def __init__(self):
        self.ops = []
        self.lastw = {}
        self.rd = {}
        self.rd_dma = {}

    def add(self, eng, fn, r=(), w=(), dma=False):
        op = Op()
        op.eng = eng
        op.fn = fn
        op.is_dma = dma
        op.sig = False
        op.cnt = 0
        deps = {}
        for k in r:
            o = self.lastw.get(k)
            if o is not None:
                deps[id(o)] = o
        for k in w:
            o = self.lastw.get(k)
            if o is not None:
                deps[id(o)] = o
            for o in self.rd.get(k, {}).values():
                deps[id(o)] = o
            for o in self.rd_dma.get(k, ()):
                deps[id(o)] = o
        op.deps = list(deps.values())
        for k in r:
            if dma:
                self.rd_dma.setdefault(k, []).append(op)
            else:
                self.rd.setdefault(k, {})[eng] = op
        for k in w:
            self.lastw[k] = op
            self.rd[k] = {}
            self.rd_dma[k] = []
        self.ops.append(op)
        return op

    def finalize(self):
        qcount = {}
        for op in self.ops:
            for d in op.deps:
                if d.is_dma:
                    continue
                if d.eng != op.eng or op.is_dma or d.eng != "pe":
                    d.sig = True
            if op.is_dma:
                qi = qcount.get(op.eng, 0)
                qcount[op.eng] = qi + 1
                op.qi = qi
        cnt = {}
        for op in self.ops:
            if op.sig and not op.is_dma:
                cnt[op.eng] = cnt.get(op.eng, 0) + 1
                op.cnt = cnt[op.eng]

    def emit_engine(self, eng, handle, sems, dma_sems):
        waited = {}
        pool = dma_sems.get(eng)
        for op in self.ops:
            if op.eng != eng:
                continue
            for d in op.deps:
                if d.is_dma:
                    s = dma_sems[d.eng][d.qi % self.NSEM]
                    v = 16 * (d.qi // self.NSEM + 1)
                elif d.eng == eng and eng == "pe" and not op.is_dma:
                    continue
                else:
                    s = sems[d.eng]
                    v = d.cnt
                key = id(s)
                if waited.get(key, 0) >= v:
                    continue
                handle.wait_ge(s, v)
                waited[key] = v
            if op.is_dma:
                s = pool[op.qi % self.NSEM]
                if op.qi >= self.NSEM:
                    v = 16 * (op.qi // self.NSEM)
                    if waited.get(id(s), 0) < v:
                        handle.wait_ge(s, v)
                        waited[id(s)] = v
                ins = op.fn(handle)
                ins.then_inc(s, 16)
            else:
                if op.fn is None:
                    continue
                ins = op.fn(handle)
                if op.sig:
                    ins.then_inc(sems[eng], 1)


class Arena:
    def __init__(self, ap, nwords):
        self.ap = ap
        self.n = nwords
        self.off = 0

    def alloc(self, dtype, shape):
        n = 1
        for s in shape:
            n *= s
        words = n if dtype in (F32, I32) else (n + 1) // 2
        a = self.ap[:, self.off:self.off + words]
        self.off += words
        assert self.off <= self.n, f"arena overflow {self.off} > {self.n}"
        if dtype == BF16:
            a = a.bitcast(BF16)
        elif dtype == I32:
            a = a.bitcast(I32)
        if len(shape) == 2:
            a = a.rearrange("p (a b) -> p a b", a=shape[0])
        elif len(shape) == 3:
            a = a.rearrange("p (a b c) -> p a b c", a=shape[0], b=shape[1])
        return a


ARENA_WORDS = 53200


class Builder:
    def __init__(self, cfg):
        self.cfg = cfg
        self.nlayers = cfg.get("layers", L)
        self.ntiles = cfg.get("tiles", NT)
        self.debug = cfg.get("debug", False)
        self.stop = cfg.get("stop", None)
        self.S = Sched()
        self.bank_rr = 0

    def declare_dram(self, nc):
        def inp(name, shape, dt=F32):
            return nc.dram_tensor(name, shape, dt, kind="ExternalInput").ap()

        self.xT = inp("xT", [D, TOK])
        self.pos = inp("pos", [1, TOK], I32)
        NL = self.nlayers
        self.w_ada = inp("w_ada", [NL, D, 9 * D])
        self.wgu = [inp("wgu1", [NL, D, 2 * DFF]), inp("wgu2", [NL, D, 2 * DFF])]
        self.wd = [inp("wd1", [NL, DFF, D]), inp("wd2", [NL, DFF, D])]
        self.w_in = inp("w_in_p", [NL, D, WIN_COLS])
        self.w_uq = inp("w_uq_p", [NL, 384, WUQ_COLS])
        self.w_ukv = inp("w_ukv_p", [NL, 256, 1024])
        self.w_o = inp("w_o", [NL, D, D])
        self.vecs = inp("vecs", [128, NV])
        self.consts = inp("consts", [128, 384])
        self.yT = nc.dram_tensor("yT", [D, TOK], F32, kind="ExternalOutput").ap()
        kind = "ExternalOutput" if self.debug else "Internal"

        def scr(name, shape, dt):
            return nc.dram_tensor(name, shape, dt, kind=kind).ap()

        self.XW = scr("XW", [D, TOK], F32)
        self.QD = scr("QD", [NT, 96, NH * T], BF16)
        self.KD = scr("KD", [NT, 96, NH * T], BF16)
        self.VD = scr("VD", [NT, 128, 4 * 768], BF16)
        self.CD = scr("CD", [NT, 128, 4 * T], BF16)
        self.CT = scr("CT", [32, TOK], F32)
        self.ST = scr("ST", [32, TOK], F32)

    def bank(self):
        b = 1 + (self.bank_rr % 7)
        self.bank_rr += 1
        return b

    def barrier(self):
        self.S.add("pool", lambda e: e.memset(self.SCR1[0:1, 0:1], 0.0), r=(), w=["ARENA"])

    def add(self, eng, fn, r=(), w=(), dma=False):
        r = tuple(r) + ("ARENA",)
        return self.S.add(eng, fn, r=r, w=w, dma=dma)

    def dump(self, name, ap, shape, dt, keys):
        if not self.debug:
            return
        d = self.nc.dram_tensor("DBG_" + name, list(shape), dt, kind="ExternalOutput").ap()
        self.add("sp", lambda e: e.dma_start(out=d, in_=ap), r=keys, w=["DBG_" + name], dma=True)

    def par(self, l, q, k, b):
        return self.PAR[:, ((l * 9 + q) * 8 + k) * 2 + b:((l * 9 + q) * 8 + k) * 2 + b + 1]

    def vcol(self, c, p0=0, p1=128):
        return self.VEC[p0:p1, c:c + 1]

    def rms_stats(self, chunks, nfeat, RS, rs_key, SQ, kparts=128):
        ps0 = self.PS[:, 0, :]
        n = len(chunks)
        for i, (ap, keys) in enumerate(chunks):
            sq = SQ[0:kparts, i % 2, :]
            self.add("act", lambda e, sq=sq, ap=ap: e.activation(sq, ap, AF.Square),
                     r=keys, w=[f"SQ{i % 2}"])
            self.add("pe", lambda e, sq=sq, i=i: e.matmul(ps0, lhsT=self.ONESB[0:kparts, :], rhs=sq,
                                                          start=(i == 0), stop=(i == n - 1)),
                     r=[f"SQ{i % 2}"], w=["ps0"])
        self.rstd_from_psum(ps0, "ps0", nfeat, RS, rs_key)

    def rstd_from_psum(self, ps, ps_key, nfeat, RS, rs_key):
        self.add("act", lambda e: e.activation(RS, ps, AF.Ln, bias=self.vcol(V_EPS), scale=1.0 / nfeat),
                 r=[ps_key, "VEC"], w=[rs_key])
        self.add("act", lambda e: e.activation(RS, RS, AF.Exp, scale=-0.5), r=[rs_key], w=[rs_key])

    def prep(self):
        A = self.arena
        base = A.off
        CACT = A.alloc(BF16, (16,))
        WA = [A.alloc(BF16, (8, 3072)) for _ in range(2)]
        MOD = A.alloc(F32, (72, 2))
        TMPM = A.alloc(F32, (8, 2))
        POSI = A.alloc(I32, (2048,))
        ANG = A.alloc(F32, (2048,))
        KF = A.alloc(F32, (2048,))
        RR = A.alloc(F32, (2048,))
        CC = A.alloc(F32, (2048,))
        SS = A.alloc(F32, (2048,))
        add = self.add
        VEC = self.VEC
        add("sp", lambda e: e.dma_start(out=VEC, in_=self.vecs[:, :]), w=["VEC"], dma=True)
        add("pool", lambda e: e.dma_start(out=self.ONESB, in_=self.consts[:, 0:128]), w=["ONESB"], dma=True)
        add("pool", lambda e: e.dma_start(out=self.IDN, in_=self.consts[:, 128:256]), w=["IDN"], dma=True)
        add("pool", lambda e: e.dma_start(out=self.MB, in_=self.consts[:, 256:384]), w=["MB"], dma=True)
        add("sp", lambda e: e.dma_start(out=self.ONESF, in_=self.consts[:, 0:128]), w=["ONESF"], dma=True)
        add("act", lambda e: e.activation(CACT, VEC[:, V_CT:V_CT + 16], AF.Silu), r=["VEC"], w=["CACT"])
        psA = self.PS[:, 1, :]
        blk_i = 0
        for l in range(self.nlayers):
            for blk in range(3):
                wa = WA[blk_i % 2]
                wk = f"WA{blk_i % 2}"
                blk_i += 1
                for k in range(8):
                    add("pool", lambda e, wa=wa, k=k, l=l, blk=blk: e.dma_start(
                        out=wa[:, k, :], in_=self.w_ada[l, k * 128:(k + 1) * 128, blk * 3072:(blk + 1) * 3072]),
                        w=[f"{wk}.{k}"], dma=True)
                for j in range(24):
                    jj = blk * 24 + j
                    for k in range(8):
                        add("pe", lambda e, wa=wa, k=k, j=j, jj=jj: e.matmul(
                            psA[:, 2 * jj:2 * jj + 2], lhsT=wa[:, k, j * 128:(j + 1) * 128],
                            rhs=CACT[:, 2 * k:2 * k + 2], start=(k == 0), stop=(k == 7)),
                            r=[f"{wk}.{k}", "CACT"], w=["psA"])
            bada = VEC[:, V_BADA + 72 * l:V_BADA + 72 * (l + 1)]
            add("dve", lambda e, bada=bada: e.tensor_tensor(
                out=MOD, in0=psA[:, 0:144].rearrange("p (a b) -> p a b", b=2),
                in1=bada.rearrange("p (a b) -> p a b", b=1).to_broadcast([128, 72, 2]), op=ALU.add),
                r=["psA", "VEC"], w=["MOD"])
            norm_offs = [V_FFN1N, V_MIXN, V_FFN2N]
            coefs = [0.5, 1.0, 0.5]
            for i in range(3):
                nrm = VEC[:, norm_offs[i] + 8 * l:norm_offs[i] + 8 * (l + 1)]
                pa = self.PAR[:, ((l * 9 + 3 * i) * 16):((l * 9 + 3 * i) * 16) + 16].rearrange("p (a b) -> p a b", b=2)
                pb = self.PAR[:, ((l * 9 + 3 * i + 1) * 16):((l * 9 + 3 * i + 1) * 16) + 16].rearrange("p (a b) -> p a b", b=2)
                pg = self.PAR[:, ((l * 9 + 3 * i + 2) * 16):((l * 9 + 3 * i + 2) * 16) + 16].rearrange("p (a b) -> p a b", b=2)
                sh = MOD[:, (3 * i) * 8:(3 * i + 1) * 8, :]
                sc = MOD[:, (3 * i + 1) * 8:(3 * i + 2) * 8, :]
                gt = MOD[:, (3 * i + 2) * 8:(3 * i + 3) * 8, :]
                add("dve", lambda e, sc=sc: e.tensor_scalar(out=TMPM, in0=sc, scalar1=1.0, scalar2=None, op0=ALU.add),
                    r=["MOD"], w=["TMPM"])
                add("dve", lambda e, pa=pa, nrm=nrm: e.tensor_tensor(
                    out=pa, in0=TMPM, in1=nrm.rearrange("p (a b) -> p a b", b=1).to_broadcast([128, 8, 2]), op=ALU.mult),
                    r=["TMPM", "VEC"], w=["PAR"])
                add("dve", lambda e, pb=pb, sh=sh: e.tensor_copy(out=pb, in_=sh), r=["MOD"], w=["PAR"])
                add("dve", lambda e, pg=pg, gt=gt, cf=coefs[i]: e.tensor_scalar(
                    out=pg, in0=gt, scalar1=1.0, scalar2=cf, op0=ALU.add, op1=ALU.mult), r=["MOD"], w=["PAR"])
        P0, P1 = 64, 96
        ncols = self.ntiles * T
        for c0 in range(0, ncols, 2048):
            n = min(2048, ncols - c0)
            add("sp", lambda e, c0=c0, n=n: e.dma_start(out=POSI[P0:P1, 0:n],
                                                        in_=self.pos[0:1, c0:c0 + n].partition_broadcast(32)),
                w=["POSI"], dma=True)
            add("dve", lambda e, n=n: e.tensor_copy(out=ANG[P0:P1, 0:n], in_=POSI[P0:P1, 0:n]), r=["POSI"], w=["ANG"])
            add("dve", lambda e, n=n: e.tensor_scalar(out=ANG[P0:P1, 0:n], in0=ANG[P0:P1, 0:n],
                                                      scalar1=self.vcol(V_INVF, P0, P1), scalar2=None, op0=ALU.mult),
                r=["ANG", "VEC"], w=["ANG"])
            add("dve", lambda e, n=n: e.tensor_scalar(out=KF[P0:P1, 0:n], in0=ANG[P0:P1, 0:n],
                                                      scalar1=float(1.0 / TWO_PI), scalar2=MAGIC, op0=ALU.mult, op1=ALU.add),
                r=["ANG"], w=["KF"])
            add("dve", lambda e, n=n: e.tensor_scalar(out=KF[P0:P1, 0:n], in0=KF[P0:P1, 0:n],
                                                      scalar1=-MAGIC, scalar2=None, op0=ALU.add),
                r=["KF"], w=["KF"])
            add("dve", lambda e, n=n: e.scalar_tensor_tensor(out=RR[P0:P1, 0:n], in0=KF[P0:P1, 0:n], scalar=-C1,
                                                             in1=ANG[P0:P1, 0:n], op0=ALU.mult, op1=ALU.add),
                r=["KF", "ANG"], w=["RR"])
            add("dve", lambda e, n=n: e.scalar_tensor_tensor(out=RR[P0:P1, 0:n], in0=KF[P0:P1, 0:n], scalar=-C2,
                                                             in1=RR[P0:P1, 0:n], op0=ALU.mult, op1=ALU.add),
                r=["KF", "RR"], w=["RR"])
            add("dve", lambda e, n=n: e.tensor_scalar(out=RR[P0:P1, 0:n], in0=RR[P0:P1, 0:n],
                                                      scalar1=-PI_LO, scalar2=PI_LO, op0=ALU.max, op1=ALU.min),
                r=["RR"], w=["RR"])
            add("act", lambda e, n=n: e.activation(SS[P0:P1, 0:n], RR[P0:P1, 0:n], AF.Sin,
                                                   scale=self.vcol(V_SGN, P0, P1)),
                r=["RR", "VEC"], w=["SS"])
            add("act", lambda e, n=n: e.activation(CC[P0:P1, 0:n], RR[P0:P1, 0:n], AF.Sin, scale=0.5),
                r=["RR"], w=["CC"])
            add("dve", lambda e, n=n: e.tensor_tensor(out=CC[P0:P1, 0:n], in0=CC[P0:P1, 0:n], in1=CC[P0:P1, 0:n],
                                                      op=ALU.mult), r=["CC"], w=["CC"])
            add("dve", lambda e, n=n: e.tensor_scalar(out=CC[P0:P1, 0:n], in0=CC[P0:P1, 0:n],
                                                      scalar1=-2.0, scalar2=1.0, op0=ALU.mult, op1=ALU.add),
                r=["CC"], w=["CC"])
            add("sp", lambda e, c0=c0, n=n: e.dma_start(out=self.CT[:, c0:c0 + n], in_=CC[P0:P1, 0:n]),
                r=["CC"], w=["CTd"], dma=True)
            add("sp", lambda e, c0=c0, n=n: e.dma_start(out=self.ST[:, c0:c0 + n], in_=SS[P0:P1, 0:n]),
                r=["SS"], w=["STd"], dma=True)
        self.dump("PAR", self.PAR, (128, L * 9 * 16), F32, ["PAR"])
        self.dump("CACT", CACT, (128, 16), BF16, ["CACT"])
        A.off = base

    def norm_to_h(self, X, xk, H, l, sub, b, RS, SQ, TMP):
        add = self.add
        self.rms_stats([(X[:, k, :], [f"{xk}.{k}"]) for k in range(8)], D, RS, "RS", SQ)
        for k in range(8):
            tmp = TMP[:, k % 2, :]
            add("dve", lambda e, tmp=tmp, k=k: e.scalar_tensor_tensor(
                out=tmp, in0=X[:, k, :], scalar=self.par(l, 3 * sub, k, b), in1=RS, op0=ALU.mult, op1=ALU.mult),
                r=[f"{xk}.{k}", "RS", "PAR"], w=[f"TMP{k % 2}"])
            add("act", lambda e, tmp=tmp, k=k: e.activation(
                H[:, k, :], tmp, AF.Identity, bias=self.par(l, 3 * sub + 1, k, b)),
                r=[f"TMP{k % 2}", "PAR"], w=[f"H.{k}"])

    def f_pass(self, l, which, src, dst, final=False):
        A = self.arena
        base = A.off
        add = self.add
        WGU = A.alloc(BF16, (8, 2 * DFF))
        WD = A.alloc(BF16, (NF, D))
        XB = [A.alloc(F32, (8, T)) for _ in range(2)]
        H = A.alloc(BF16, (8, T))
        ACTB = A.alloc(BF16, (NF, T))
        SQ = A.alloc(BF16, (2, T))
        RS = A.alloc(F32, (T,))
        TMP = A.alloc(F32, (2, T))
        sub = 0 if which == 0 else 2
        wgu = self.wgu[which]
        wd = self.wd[which]
        wguv = wgu[l].rearrange("(k p) n -> p k n", p=128)
        for f0 in range(0, NF, 6):
            f1 = min(NF, f0 + 6)
            for half in range(2):
                c0 = half * DFF + f0 * 128
                c1 = half * DFF + f1 * 128
                add("pool", lambda e, c0=c0, c1=c1: e.dma_start(out=WGU[:, :, c0:c1], in_=wguv[:, :, c0:c1]),
                    w=[f"WGU{half}.{f}" for f in range(f0, f1)], dma=True)
        for f0 in range(0, NF, 2):
            add("pool", lambda e, f0=f0: e.dma_start(
                out=WD[:, f0:f0 + 2, :],
                in_=wd[l, f0 * 128:(f0 + 2) * 128, :].rearrange("(f p) d -> p f d", p=128)),
                w=[f"WD.{f0}", f"WD.{f0 + 1}"], dma=True)
        srcv = src.rearrange("(k p) t -> p k t", p=128)
        dstv = dst.rearrange("(k p) t -> p k t", p=128)
        PS = self.PS

        def load(t):
            xb = t % 2
            add("sp", lambda e: e.dma_start(out=XB[xb], in_=srcv[:, :, t * T:(t + 1) * T]),
                r=[f"XD.{t}"], w=[f"X{xb}.{k}" for k in range(8)], dma=True)

        load(0)

        def do_tile(t):
            xb = t % 2
            X = XB[xb]
            xk = f"X{xb}"
            b = t // NTS
            if t + 1 < self.ntiles:
                load(t + 1)
            if t == 0:
                self.norm_to_h(X, xk, H, l, sub, b, RS, SQ, TMP)
            for f in range(NF):
                g = f % 2
                for k in range(8):
                    add("pe", lambda e, f=f, k=k, g=g: e.matmul(
                        PS[:, 1 + g, :], lhsT=WGU[:, k, f * 128:(f + 1) * 128], rhs=H[:, k, :],
                        start=(k == 0), stop=(k == 7)), r=[f"WGU0.{f}", f"H.{k}"], w=[f"psG{g}"])
                for k in range(8):
                    add("pe", lambda e, f=f, k=k, g=g: e.matmul(
                        PS[:, 3 + g, :], lhsT=WGU[:, k, DFF + f * 128:DFF + (f + 1) * 128], rhs=H[:, k, :],
                        start=(k == 0), stop=(k == 7)), r=[f"WGU1.{f}", f"H.{k}"], w=[f"psU{g}"])
                add("act", lambda e, g=g: e.activation(TMP[:, g, :], PS[:, 1 + g, :], AF.Silu),
                    r=[f"psG{g}"], w=[f"TMP{g}"])
                add("dve", lambda e, f=f, g=g: e.tensor_tensor(out=ACTB[:, f, :], in0=TMP[:, g, :], in1=PS[:, 3 + g, :],
                                                               op=ALU.mult),
                    r=[f"TMP{g}", f"psU{g}"], w=[f"ACTB.{f}"])
            if t + 1 < self.ntiles:
                self.norm_to_h(XB[(t + 1) % 2], f"X{(t + 1) % 2}", H, l, sub, (t + 1) // NTS, RS, SQ, TMP)
            for d in range(8):
                yb = d % 2
                for f in range(NF):
                    add("pe", lambda e, f=f, d=d, yb=yb: e.matmul(
                        PS[:, 5 + yb, :], lhsT=WD[:, f, d * 128:(d + 1) * 128], rhs=ACTB[:, f, :],
                        start=(f == 0), stop=(f == NF - 1)), r=[f"WD.{f}", f"ACTB.{f}"], w=[f"psY{yb}"])
                add("dve", lambda e, d=d, yb=yb: e.scalar_tensor_tensor(
                    out=X[:, d, :], in0=PS[:, 5 + yb, :], scalar=self.par(l, 3 * sub + 2, d, b), in1=X[:, d, :],
                    op0=ALU.mult, op1=ALU.add), r=[f"psY{yb}", f"{xk}.{d}", "PAR"], w=[f"{xk}.{d}"])
            if final:
                self.rms_stats([(X[:, k, :], [f"{xk}.{k}"]) for k in range(8)], D, RS, "RS", SQ)
                for k in range(8):
                    add("dve", lambda e, k=k: e.scalar_tensor_tensor(
                        out=X[:, k, :], in0=X[:, k, :], scalar=self.vcol(V_FINALN + k), in1=RS,
                        op0=ALU.mult, op1=ALU.mult), r=[f"{xk}.{k}", "RS", "VEC"], w=[f"{xk}.{k}"])
            o = add("sp", lambda e, t=t, X=X: e.dma_start(out=dstv[:, :, t * T:(t + 1) * T], in_=X),
                    r=[f"{xk}.{k}" for k in range(8)], w=[f"XD.{t}"], dma=True)
            if final:
                self.out_dmas.append(o)

        for t in range(self.ntiles):
            do_tile(t)
        A.off = base

    def m1_pass(self, l):
        A = self.arena
        base = A.off
        add = self.add
        PS = self.PS
        WIN = A.alloc(BF16, (8, WIN_COLS))
        WUQ = A.alloc(BF16, (3, WUQ_COLS))
        WUKV = A.alloc(BF16, (2, 1024))
        XB = [A.alloc(F32, (8, T)) for _ in range(2)]
        H = A.alloc(BF16, (8, T))
        CQN = A.alloc(BF16, (3, T))
        CKVN = A.alloc(BF16, (2, T))
        CS = A.alloc(F32, (2, T))
        SQ = A.alloc(BF16, (2, T))
        RS = A.alloc(F32, (T,))
        RS2 = A.alloc(F32, (T,))
        TMP = A.alloc(F32, (2, T))
        U = A.alloc(F32, (4, T + 2))
        GB = A.alloc(F32, (4, T))
        CV = A.alloc(F32, (4, T))
        YC = A.alloc(F32, (2, T))
        CG = A.alloc(BF16, (4, T))
        T1B = A.alloc(F32, (2, T))
        T2B = A.alloc(F32, (2, T))
        KR = A.alloc(BF16, (T,))
        QS = A.alloc(BF16, (NH, T))
        KS = A.alloc(BF16, (NH, T))
        VS = A.alloc(BF16, (4, 4, 192))
        for k in range(8):
            add("pool", lambda e, k=k: e.dma_start(out=WIN[:, k, :], in_=self.w_in[l, k * 128:(k + 1) * 128, :]),
                w=[f"WIN.{k}"], dma=True)
        add("pool", lambda e: e.dma_start(out=WUQ, in_=self.w_uq[l].rearrange("(k p) n -> p k n", p=128)),
            w=["WUQ"], dma=True)
        add("pool", lambda e: e.dma_start(out=WUKV, in_=self.w_ukv[l].rearrange("(k p) n -> p k n", p=128)),
            w=["WUKV"], dma=True)
        add("dve", lambda e: e.memset(VS, 1.0), w=["VS"])
        srcv = self.XW.rearrange("(k p) t -> p k t", p=128)

        def load(t):
            xb = t % 2
            add("sp", lambda e: e.dma_start(out=XB[xb], in_=srcv[:, :, t * T:(t + 1) * T]),
                r=[f"XD.{t}"], w=[f"X{xb}.{k}" for k in range(8)], dma=True)

        def proj(col0, ncol, bank, tag):
            for k in range(8):
                add("pe", lambda e, k=k: e.matmul(PS[0:ncol, bank, :], lhsT=WIN[:, k, col0:col0 + ncol], rhs=H[:, k, :],
                                                  start=(k == 0), stop=(k == 7)),
                    r=[f"WIN.{k}", f"H.{k}"], w=[f"ps{bank}"])

        load(0)

        def do_tile(t):
            xb = t % 2
            X = XB[xb]
            xk = f"X{xb}"
            b = t // NTS
            j = t % NTS
            if t + 1 < self.ntiles:
                load(t + 1)
            add("sp", lambda e, t=t: e.dma_start(out=CS[64:96, 0, :], in_=self.CT[:, t * T:(t + 1) * T]),
                r=["CTd"], w=["CS0"], dma=True)
            add("sp", lambda e, t=t: e.dma_start(out=CS[64:96, 1, :], in_=self.ST[:, t * T:(t + 1) * T]),
                r=["STd"], w=["CS1"], dma=True)
            if t == 0:
                self.norm_to_h(X, xk, H, l, 1, b, RS, SQ, TMP)
            for (col0, nch, DST, dk, gcol, nfeat) in ((0, 3, CQN, "CQN", V_QAN + 3 * l, 384),
                                                       (384, 2, CKVN, "CKVN", V_KVAN + 2 * l, 256)):
                banks = []
                for m in range(nch):
                    bk = self.bank()
                    banks.append(bk)
                    proj(col0 + m * 128, 128, bk, dk)
                self.rms_stats([(PS[:, bk, :], [f"ps{bk}"]) for bk in banks], nfeat, RS, "RS", SQ)
                for m, bk in enumerate(banks):
                    add("dve", lambda e, m=m, bk=bk, DST=DST, gcol=gcol: e.scalar_tensor_tensor(
                        out=DST[:, m, :], in0=PS[:, bk, :], scalar=self.vcol(gcol + m), in1=RS,
                        op0=ALU.mult, op1=ALU.mult), r=[f"ps{bk}", "RS", "VEC"], w=[f"{dk}.{m}"])
            bk1 = self.bank()
            proj(2176, 96, bk1, "kpe")
            bk2 = self.bank()
            proj(2272, 96, bk2, "kpesw")
            add("dve", lambda e, bk1=bk1: e.tensor_tensor(out=T1B[64:96, 0, :], in0=PS[64:96, bk1, :], in1=CS[64:96, 0, :],
                                                           op=ALU.mult), r=[f"ps{bk1}", "CS0"], w=["T1.0"])
            add("dve", lambda e, bk2=bk2: e.tensor_tensor(out=T2B[64:96, 0, :], in0=PS[64:96, bk2, :], in1=CS[64:96, 1, :],
                                                           op=ALU.mult), r=[f"ps{bk2}", "CS1"], w=["T2.0"])
            add("dve", lambda e: e.tensor_tensor(out=KR[64:96, :], in0=T1B[64:96, 0, :], in1=T2B[64:96, 0, :], op=ALU.add),
                r=["T1.0", "T2.0"], w=["KR"])
            add("sp", lambda e, t=t: e.dma_start(
                out=self.KD[t].rearrange("p (a b) -> p a b", a=NH)[64:96],
                in_=KR[64:96, None, :].to_broadcast([32, NH, T])),
                r=["KR"], w=[f"KDr.{t}"], dma=True)
            if j == 0:
                add("pool", lambda e: e.memset(U[:, :, 0:2], 0.0), w=[f"U.{c}" for c in range(4)])
            for c in range(4):
                bgc = self.bank()
                proj(640 + 512 + c * 128, 128, bgc, "gc")
                bvl = self.bank()
                proj(640 + 1024 + c * 128, 128, bvl, "val")
                bgb = self.bank()
                proj(640 + c * 128, 128, bgb, "gb")
                add("act", lambda e, bgc=bgc: e.activation(TMP[:, 0, :], PS[:, bgc, :], AF.Copy),
                    r=[f"ps{bgc}"], w=["TMP0"])
                add("dve", lambda e, c=c, bvl=bvl: e.tensor_tensor(out=U[:, c, 2:T + 2], in0=TMP[:, 0, :],
                                                                   in1=PS[:, bvl, :], op=ALU.mult),
                    r=["TMP0", f"ps{bvl}"], w=[f"U.{c}"])
                add("act", lambda e, c=c, bgb=bgb: e.activation(GB[:, c, :], PS[:, bgb, :], AF.Copy),
                    r=[f"ps{bgb}"], w=[f"GB.{c}"])
                yc = YC[:, c % 2, :]
                wc = V_CONVW + 12 * l
                add("pool", lambda e, c=c, yc=yc, wc=wc: e.tensor_scalar(
                    out=yc, in0=U[:, c, 2:T + 2], scalar1=self.vcol(wc + 8 + c), scalar2=0.0, op0=ALU.mult, op1=ALU.add),
                    r=[f"U.{c}", "VEC"], w=[f"YC{c % 2}"])
                add("dve", lambda e, c=c, yc=yc, wc=wc: e.scalar_tensor_tensor(
                    out=yc, in0=U[:, c, 1:T + 1], scalar=self.vcol(wc + 4 + c), in1=yc, op0=ALU.mult, op1=ALU.add),
                    r=[f"U.{c}", "VEC", f"YC{c % 2}"], w=[f"YC{c % 2}"])
                add("dve", lambda e, c=c, yc=yc, wc=wc: e.scalar_tensor_tensor(
                    out=yc, in0=U[:, c, 0:T], scalar=self.vcol(wc + c), in1=yc, op0=ALU.mult, op1=ALU.add),
                    r=[f"U.{c}", "VEC", f"YC{c % 2}"], w=[f"YC{c % 2}"])
                add("pool", lambda e, c=c, yc=yc: e.tensor_tensor(out=CV[:, c, :], in0=yc, in1=GB[:, c, :], op=ALU.mult),
                    r=[f"YC{c % 2}", f"GB.{c}"], w=[f"CV.{c}"])
                add("pool", lambda e, c=c: e.tensor_copy(out=U[:, c, 0:2], in_=U[:, c, T:T + 2]),
                    r=[f"U.{c}"], w=[f"U.{c}"])
            if t + 1 < self.ntiles:
                self.norm_to_h(XB[(t + 1) % 2], f"X{(t + 1) % 2}", H, l, 1, (t + 1) // NTS, RS, SQ, TMP)
            for h in range(NH):
                ba = self.bank()
                for k in range(3):
                    add("pe", lambda e, k=k, h=h, ba=ba: e.matmul(
                        PS[0:96, ba, :], lhsT=WUQ[:, k, h * 192:h * 192 + 96], rhs=CQN[:, k, :],
                        start=(k == 0), stop=(k == 2)), r=["WUQ", f"CQN.{k}"], w=[f"ps{ba}"])
                bb = self.bank()
                for k in range(3):
                    add("pe", lambda e, k=k, h=h, bb=bb: e.matmul(
                        PS[0:96, bb, :], lhsT=WUQ[:, k, h * 192 + 96:h * 192 + 192], rhs=CQN[:, k, :],
                        start=(k == 0), stop=(k == 2)), r=["WUQ", f"CQN.{k}"], w=[f"ps{bb}"])
                add("act", lambda e, h=h, ba=ba: e.activation(QS[0:64, h, :], PS[0:64, ba, :], AF.Copy),
                    r=[f"ps{ba}"], w=["QS"])
                tb = h % 2
                add("dve", lambda e, ba=ba, tb=tb: e.tensor_tensor(out=T1B[64:96, tb, :], in0=PS[64:96, ba, :],
                                                                    in1=CS[64:96, 0, :], op=ALU.mult),
                    r=[f"ps{ba}", "CS0"], w=[f"T1.{tb}"])
                add("dve", lambda e, bb=bb, tb=tb: e.tensor_tensor(out=T2B[64:96, tb, :], in0=PS[64:96, bb, :],
                                                                    in1=CS[64:96, 1, :], op=ALU.mult),
                    r=[f"ps{bb}", "CS1"], w=[f"T2.{tb}"])
                add("pool", lambda e, h=h, tb=tb: e.tensor_tensor(out=QS[64:96, h, :], in0=T1B[64:96, tb, :],
                                                                   in1=T2B[64:96, tb, :], op=ALU.add),
                    r=[f"T1.{tb}", f"T2.{tb}"], w=["QS"])
            for h in range(NH):
                bk = self.bank()
                for k in range(2):
                    add("pe", lambda e, k=k, h=h, bk=bk: e.matmul(
                        PS[0:64, bk, :], lhsT=WUKV[:, k, h * 64:(h + 1) * 64], rhs=CKVN[:, k, :],
                        start=(k == 0), stop=(k == 1)), r=["WUKV", f"CKVN.{k}"], w=[f"ps{bk}"])
                add("act", lambda e, h=h, bk=bk: e.activation(KS[0:64, h, :], PS[0:64, bk, :], AF.Copy),
                    r=[f"ps{bk}"], w=["KSn"])
            for c4 in range(4):
                bk = self.bank()
                for k in range(2):
                    add("pe", lambda e, k=k, c4=c4, bk=bk: e.matmul(
                        PS[:, bk, :], lhsT=CKVN[:, k, c4 * 128:(c4 + 1) * 128], rhs=WUKV[:, k, 512:1024],
                        start=(k == 0), stop=(k == 1)), r=["WUKV", f"CKVN.{k}"], w=[f"ps{bk}"])
                pv4 = PS[:, bk, :].rearrange("p (a s d) -> p a s d", a=4, s=2)
                add("dve", lambda e, c4=c4, pv4=pv4: e.tensor_copy(out=VS[:, c4, :, 0:64], in_=pv4[:, :, 0, :]),
                    r=[f"ps{bk}"], w=["VS"])
                add("act", lambda e, c4=c4, pv4=pv4: e.activation(VS[:, c4, :, 128:192], pv4[:, :, 1, :], AF.Copy),
                    r=[f"ps{bk}"], w=["VS"])
            self.rms_stats([(CV[:, c, :], [f"CV.{c}"]) for c in range(4)], 512, RS2, "RS2", SQ)
            for c in range(4):
                add("dve", lambda e, c=c: e.scalar_tensor_tensor(
                    out=CG[:, c, :], in0=CV[:, c, :], scalar=self.vcol(V_CON + 4 * l + c), in1=RS2,
                    op0=ALU.mult, op1=ALU.mult), r=[f"CV.{c}", "VEC", "RS2"], w=["CG"])
            add("sp", lambda e, t=t: e.dma_start(out=self.CD[t].rearrange("p (a b) -> p a b", a=4), in_=CG),
                r=["CG"], w=[f"CDd.{t}"], dma=True)
            add("sp", lambda e, t=t: e.dma_start(out=self.QD[t].rearrange("p (a b) -> p a b", a=NH), in_=QS[0:96]),
                r=["QS"], w=[f"QDd.{t}"], dma=True)
            add("sp", lambda e, t=t: e.dma_start(out=self.KD[t].rearrange("p (a b) -> p a b", a=NH)[0:64], in_=KS[0:64]),
                r=["KSn"], w=[f"KDd.{t}"], dma=True)
            add("sp", lambda e, t=t: e.dma_start(out=self.VD[t].rearrange("p (a b c) -> p a b c", a=4, b=4), in_=VS),
                r=["VS"], w=[f"VDd.{t}"], dma=True)

        for t in range(self.ntiles):
            do_tile(t)
        A.off = base

    def m2_pass(self, l):
        A = self.arena
        base = A.off
        add = self.add
        PS = self.PS
        KT = A.alloc(BF16, (NH, SEQ))
        VT = A.alloc(BF16, (32, 768))
        WO = A.alloc(BF16, (8, D))
        QTB = [A.alloc(BF16, (NH, T)) for _ in range(2)]
        CGB = [A.alloc(BF16, (4, T)) for _ in range(2)]
        X = A.alloc(F32, (8, T))
        ANF = A.alloc(F32, (4, T))
        AG = A.alloc(BF16, (4, T))
        PT = A.alloc(BF16, (4, T))
        RBS = A.alloc(F32, (T,))
        SQ = A.alloc(BF16, (4, T))
        RSA = A.alloc(F32, (T,))
        add("pool", lambda e: e.dma_start(out=WO, in_=self.w_o[l].rearrange("(k p) n -> p k n", p=128)),
            w=["WO"], dma=True)
        xv = self.XW.rearrange("(k p) t -> p k t", p=128)
        ST_B = (0, 1, 4)
        OT_B = (2, 3)
        SSA_B = 5
        Y_B = (6, 6)
        DUM_B = 7
        NQA = 2

        def load(t):
            qb = t % 2
            add("sp", lambda e: e.dma_start(out=QTB[qb][0:96], in_=self.QD[t].rearrange("p (a b) -> p a b", a=NH)),
                r=[f"QDd.{t}"], w=[f"QT{qb}"], dma=True)
            add("sp", lambda e: e.dma_start(out=CGB[qb], in_=self.CD[t].rearrange("p (a b) -> p a b", a=4)),
                r=[f"CDd.{t}"], w=[f"CGB{qb}"], dma=True)

        def load_kv(t):
            j = t % NTS
            add("sp", lambda e: e.dma_start(out=KT[0:96, :, j * T:(j + 1) * T],
                                            in_=self.KD[t].rearrange("p (a b) -> p a b", a=NH)),
                r=[f"KDd.{t}", f"KDr.{t}"], w=[f"KT.{j}"], dma=True)
            add("sp", lambda e: e.dma_start(out=VT[:, 4 * j:4 * j + 4, :],
                                            in_=self.VD[t].rearrange("p (a b) -> p a b", a=4)),
                r=[f"VDd.{t}"], w=[f"VT.{j}"], dma=True)

        carry = []

        def tile_end(t, qb, b):
            def part0():
                for pr in range(4):
                    add("pe", lambda e, pr=pr: e.matmul(PS[:, SSA_B, :], lhsT=self.ONESB, rhs=SQ[:, pr, :],
                                                        start=(pr == 0), stop=(pr == 3)),
                        r=[f"SQ{pr}", "ONESB"], w=["psSSA"])
                self.rstd_from_psum(PS[:, SSA_B, :], "psSSA", 512, RSA, "RSA")
                for pr in range(4):
                    add("dve", lambda e, pr=pr: e.scalar_tensor_tensor(
                        out=AG[:, pr, :], in0=ANF[:, pr, :], scalar=self.vcol(V_AON + 8 * l + pr), in1=RSA,
                        op0=ALU.mult, op1=ALU.mult), r=[f"ANF.{pr}", "RSA", "VEC"], w=[f"AG.{pr}"])
            carry.append(part0)

            def wo_mm(d, k):
                yb = Y_B[d % 2]
                rhs = AG[:, k, :] if k < 4 else CGB[qb][:, k - 4, :]
                rk = f"AG.{k}" if k < 4 else f"CGB{qb}"
                add("pe", lambda e: e.matmul(PS[:, yb, :], lhsT=WO[:, k, d * 128:(d + 1) * 128], rhs=rhs,
                                             start=(k == 0), stop=(k == 7)), r=["WO", rk], w=[f"psY{yb}"])
                if k == 7:
                    add("dve", lambda e: e.scalar_tensor_tensor(
                        out=X[:, d, :], in0=PS[:, yb, :], scalar=self.par(l, 5, d, b), in1=X[:, d, :],
                        op0=ALU.mult, op1=ALU.add), r=[f"psY{yb}", f"X.{d}", "PAR"], w=[f"X.{d}"])
            for d in range(8):
                for k in range(8):
                    carry.append(lambda d=d, k=k: wo_mm(d, k))

            def fin():
                add("sp", lambda e: e.dma_start(out=xv[:, :, t * T:(t + 1) * T], in_=X),
                    r=[f"X.{k}" for k in range(8)], w=[f"XD.{t}"], dma=True)
            carry.append(fin)

        def load_x(t):
            add("sp", lambda e: e.dma_start(out=X, in_=xv[:, :, t * T:(t + 1) * T]),
                r=[f"XD.{t}"], w=[f"X.{k}" for k in range(8)], dma=True)

        load(0)
        load_x(0)

        def do_tile(t):
            qb = t % 2
            QT = QTB[qb]
            b = t // NTS
            j = t % NTS
            load_kv(t)
            nch = 4 * j + 4
            blocks = [(h, c) for h in range(NH) for c in range(nch)]
            nb = len(blocks)
            quota = -(-66 // (nb - 8))
            if t == 0 and t + 1 < self.ntiles:
                load(t + 1)

            def qk(i):
                h, c = blocks[i]
                lo = max(0, c - 4 * j) * 128
                sb = ST_B[i % 3]
                diag = c >= 4 * j
                add("pe", lambda e: e.matmul(PS[:, sb, lo:T], lhsT=KT[0:96, h, c * 128:(c + 1) * 128],
                                             rhs=QT[0:96, h, lo:T], start=True, stop=not diag),
                    r=[f"KT.{c // 4}", f"QT{qb}"], w=[f"psST{i % 3}"])
                if diag:
                    add("pe", lambda e: e.matmul(PS[:, sb, lo:lo + 128], lhsT=self.IDN, rhs=self.MB,
                                                 start=False, stop=True),
                        r=["IDN", "MB"], w=[f"psST{i % 3}"])
                add("act", lambda e: e.activation(PT[:, i % 4, lo:T], PS[:, sb, lo:T], AF.Exp, scale=SM_SCALE),
                    r=[f"psST{i % 3}"], w=[f"PT{i % 4}"])

            def head_end(h):
                ob = OT_B[h % 2]
                pr = h // 2
                if h % 2 == 0:
                    o_lo, o_hi, s_lo, s_hi = 0, 64, 64, 128
                else:
                    o_lo, o_hi, s_lo, s_hi = 64, 128, 0, 64
                add("dve", lambda e: e.reciprocal(RBS[o_lo:o_hi, :], PS[s_lo:s_hi, ob, :]),
                    r=[f"psOT{h % 2}"], w=["RBS"])
                add("dve", lambda e: e.tensor_tensor(out=ANF[o_lo:o_hi, pr, :], in0=PS[o_lo:o_hi, ob, :],
                                                     in1=RBS[o_lo:o_hi, :], op=ALU.mult),
                    r=[f"psOT{h % 2}", "RBS"], w=[f"ANF.{pr}"])
                add("pool", lambda e: e.tensor_tensor(out=SQ[o_lo:o_hi, pr, :], in0=ANF[o_lo:o_hi, pr, :],
                                                      in1=ANF[o_lo:o_hi, pr, :], op=ALU.mult),
                    r=[f"ANF.{pr}"], w=[f"SQ{pr}"])

            pend = []

            def pv(i):
                h, c = blocks[i]
                lo = max(0, c - 4 * j) * 128
                ob = OT_B[h % 2]
                pr = h // 2
                c0 = 0 if h % 2 == 0 else 64
                add("pe", lambda e: e.matmul(PS[:, ob, lo:T], lhsT=VT[:, c, pr * 192 + c0:pr * 192 + c0 + 128],
                                             rhs=PT[:, i % 4, lo:T], start=(c == 0), stop=(c == nch - 1)),
                    r=[f"PT{i % 4}", f"VT.{c // 4}"], w=[f"psOT{h % 2}"])
                if c == nch - 1:
                    pend.append([3, h])

            def tick():
                for p in list(pend):
                    p[0] -= 1
                    if p[0] <= 0:
                        pend.remove(p)
                        head_end(p[1])

            for i in range(min(NQA, nb)):
                qk(i)
            for i in range(nb):
                if i + NQA < nb:
                    qk(i + NQA)
                pv(i)
                tick()
                did_filler = False
                if i >= 3 and carry:
                    nq = 1 if i == 3 else quota
                    for _ in range(nq):
                        if carry:
                            carry.pop(0)()
                            did_filler = True
                    if not carry:
                        load_x(t)
                        if t + 1 < self.ntiles:
                            load(t + 1)
                if FILLER and not did_filler:
                    add("pe", lambda e: e.matmul(PS[:, DUM_B, 0:DUMN], lhsT=self.IDN, rhs=self.IDN[:, 0:DUMN],
                                                 start=True, stop=True), r=["IDN"], w=["psDUM"])
            while pend:
                tick()
            tile_end(t, qb, b)

        for t in range(self.ntiles):
            do_tile(t)
        while carry:
            carry.pop(0)()
        A.off = base

    def build(self):
        nc = bass.Bass("TRN2", target_bir_lowering=False)
        self.nc = nc
        self.declare_dram(nc)
        self.out_dmas = []
        with nc.sbuf_tensor("arena", [128, ARENA_WORDS], F32) as arena_t, \
                nc.psum_tensor("ps", [128, 8, 512], F32) as ps_t:
            self.arena = Arena(arena_t, ARENA_WORDS)
            self.PS = ps_t
            A = self.arena
            self.VEC = A.alloc(F32, (NV,))
            self.PAR = A.alloc(F32, (L * 9 * 16,))
            self.ONESB = A.alloc(BF16, (128,))
            self.IDN = A.alloc(BF16, (128,))
            self.MB = A.alloc(BF16, (128,))
            self.ONESF = A.alloc(F32, (128,))
            self.SCR1 = A.alloc(F32, (2,))
            self.prep()
            self.barrier()
            stop = self.stop
            done = False
            for l in range(self.nlayers):
                src = self.xT if l == 0 else self.XW
                for pname in ("F1", "M1", "M2", "F2"):
                    if pname == "F1":
                        self.f_pass(l, 0, src, self.XW)
                    elif pname == "M1":
                        self.m1_pass(l)
                    elif pname == "M2":
                        self.m2_pass(l)
                    else:
                        last = (l == self.nlayers - 1)
                        self.f_pass(l, 1, self.XW, self.yT if last else self.XW, final=last)
                    self.barrier()
                    if stop is not None and stop == (l, pname):
                        done = True
                        break
                if done:
                    break
            self.S.add("sp", None, r=(), w=["ARENA"])
            self.S.finalize()
            with nc.semaphore("s_pe") as s_pe, nc.semaphore("s_act") as s_act, nc.semaphore("s_dve") as s_dve, \
                    nc.semaphore("s_pool") as s_pool, nc.semaphore("s_sp") as s_sp:
                sems = {"pe": s_pe, "act": s_act, "dve": s_dve, "pool": s_pool, "sp": s_sp}
                import contextlib
                with contextlib.ExitStack() as es:
                    dsems = {"sp": [es.enter_context(nc.semaphore(f"d_sp{i}")) for i in range(Sched.NSEM)],
                             "pool": [es.enter_context(nc.semaphore(f"d_pl{i}")) for i in range(Sched.NSEM)]}
                    with nc.Block() as block:
                        @block.tensor
                        def _(e):
                            self.S.emit_engine("pe", e, sems, dsems)

                        @block.scalar
                        def _(e):
                            self.S.emit_engine("act", e, sems, dsems)

                        @block.vector
                        def _(e):
                            self.S.emit_engine("dve", e, sems, dsems)

                        @block.gpsimd
                        def _(e):
                            self.S.emit_engine("pool", e, sems, dsems)

                        @block.sync
                        def _(e):
                            self.S.emit_engine("sp", e, sems, dsems)
        return nc


def _chunks(v, p=128):
    v = np.asarray(v, np.float32)
    return np.ascontiguousarray(v.reshape(-1, p).T)


def make_shared(inputs):
    f32 = np.float32
    w_in = np.asarray(inputs["w_in"], f32)
    cq, ckv, kpe = w_in[:, :, 0:384], w_in[:, :, 384:640], w_in[:, :, 640:672]
    gb, gc, val = w_in[:, :, 672:1184], w_in[:, :, 1184:1696], w_in[:, :, 1696:2208]
    z64 = np.zeros((L, D, 64), f32)
    kpe_sw = np.concatenate([kpe[:, :, 16:32], kpe[:, :, 0:16]], axis=-1)
    w_in_p = np.ascontiguousarray(np.concatenate([cq, ckv, gb, gc, val, z64, kpe, z64, kpe_sw], axis=-1))
    assert w_in_p.shape[-1] == WIN_COLS
    w_uq = np.asarray(inputs["w_uq"], f32).reshape(L, 384, NH, 96)
    nope, pe = w_uq[..., 0:64], w_uq[..., 64:96]
    pe_sw = np.concatenate([pe[..., 16:32], pe[..., 0:16]], axis=-1)
    zq = np.zeros((L, 384, NH, 64), f32)
    w_uq_p = np.ascontiguousarray(np.concatenate([nope, pe, zq, pe_sw], axis=-1).reshape(L, 384, WUQ_COLS))
    w_ukv = np.asarray(inputs["w_ukv"], f32).reshape(L, 256, NH, 128)
    w_ukv_p = np.ascontiguousarray(np.concatenate([w_ukv[..., 0:64].reshape(L, 256, 512),
                                                   w_ukv[..., 64:128].reshape(L, 256, 512)], axis=-1))
    consts = np.zeros((128, 384), f32)
    consts[:, 0:128] = 1.0
    consts[:, 128:256] = np.eye(128, dtype=f32)
    consts[:, 256:384] = np.where(np.arange(128)[:, None] > np.arange(128)[None, :], -30000.0, 0.0)
    shared = {
        "w_ada": np.ascontiguousarray(inputs["w_ada"], f32),
        "wgu1": np.ascontiguousarray(inputs["ffn1_w_gu"], f32), "wd1": np.ascontiguousarray(inputs["ffn1_w_down"], f32),
        "wgu2": np.ascontiguousarray(inputs["ffn2_w_gu"], f32), "wd2": np.ascontiguousarray(inputs["ffn2_w_down"], f32),
        "w_in_p": w_in_p, "w_uq_p": w_uq_p, "w_ukv_p": w_ukv_p,
        "w_o": np.ascontiguousarray(inputs["w_o"], f32), "consts": consts,
    }
    vec = np.zeros((128, NV), f32)
    for l in range(L):
        vec[:, V_BADA + 72 * l:V_BADA + 72 * (l + 1)] = _chunks(inputs["b_ada"][l])
        vec[:, V_FFN1N + 8 * l:V_FFN1N + 8 * (l + 1)] = _chunks(inputs["ffn1_norm"][l])
        vec[:, V_MIXN + 8 * l:V_MIXN + 8 * (l + 1)] = _chunks(inputs["mix_norm"][l])
        vec[:, V_FFN2N + 8 * l:V_FFN2N + 8 * (l + 1)] = _chunks(inputs["ffn2_norm"][l])
        vec[:, V_QAN + 3 * l:V_QAN + 3 * (l + 1)] = _chunks(inputs["q_a_norm"][l])
        vec[:, V_KVAN + 2 * l:V_KVAN + 2 * (l + 1)] = _chunks(inputs["kv_a_norm"][l])
        cw = np.asarray(inputs["conv_w"][l], f32)
        for jt in range(3):
            vec[:, V_CONVW + 12 * l + 4 * jt:V_CONVW + 12 * l + 4 * (jt + 1)] = _chunks(cw[jt])
        vec[:, V_AON + 8 * l:V_AON + 8 * l + 4] = _chunks(inputs["attn_out_norm"][l])
        vec[:, V_CON + 4 * l:V_CON + 4 * (l + 1)] = _chunks(inputs["conv_out_norm"][l])
    vec[:, V_FINALN:V_FINALN + 8] = _chunks(inputs["final_norm"])
    inv_freq = (1.0 / (np.float32(10000.0) ** (np.arange(0, 32, 2, dtype=f32) / np.float32(32)))).astype(f32)
    p = np.arange(128)
    vec[:, V_INVF] = inv_freq[p % 16]
    vec[:, V_SGN] = np.where((p % 32) < 16, -1.0, 1.0)
    vec[:, V_HALFPI] = np.float32(np.pi / 2)
    vec[:, V_EPS] = np.float32(EPS)
    return shared, vec


def make_core_inputs(inputs, shared, vec, i):
    f32 = np.float32
    x = np.asarray(inputs["x"][2 * i:2 * i + 2], f32)
    xT = np.ascontiguousarray(x.transpose(2, 0, 1).reshape(D, TOK))
    pos = np.ascontiguousarray(np.asarray(inputs["positions"][2 * i:2 * i + 2], np.int32).reshape(1, TOK))
    v = vec.copy()
    c = np.asarray(inputs["c"][2 * i:2 * i + 2], f32)
    v[:, V_CT:V_CT + 16] = c.reshape(2, 8, 128).transpose(2, 1, 0).reshape(128, 16)
    m = dict(shared)
    m["xT"] = xT
    m["pos"] = pos
    m["vecs"] = v
    return m


_NC_CACHE = {}


def kernel(**inputs):
    inputs = {k: np.asarray(v) for k, v in inputs.items()}
    shared, vec = make_shared(inputs)
    in_maps = [make_core_inputs(inputs, shared, vec, i) for i in range(NCORES)]
    if "nc" not in _NC_CACHE:
        _NC_CACHE["nc"] = Builder({}).build()
    nc = _NC_CACHE["nc"]
    res = run_bass_kernel_spmd(nc, in_maps, core_ids=list(range(NCORES)))
    out = np.empty((2 * NCORES, SEQ, D), np.float32)
    for i in range(NCORES):
        yT = np.asarray(res.results[i]["yT"], np.float32)
        out[2 * i:2 * i + 2] = yT.reshape(D, 2, SEQ).transpose(1, 2, 0)
    return out
```
